# Optimizing a Trainium2 kernel written in Bass

```python
import math
import numpy as np
import jax
import jax.numpy as jnp
from jax import lax

D_MODEL = 1024
BATCH = 8
SEQ = 2048
DEPTH = 1

CHUNK = 64
HEAD_DIM = 64
D_FF = 2816
EPS = 1e-6
NEG_INF = -1e30
H_A = 8
KV_A = 2
G_A = H_A // KV_A
WINDOW_A = 128
BACK_A = WINDOW_A // CHUNK
H_B = 8
BACK_B = 8
REL_CLIP = 128
N_REL = 2 * REL_CLIP + 1
QA = H_A * HEAD_DIM
KVA = KV_A * HEAD_DIM
QB = H_B * HEAD_DIM
D_IN = QA + 2 * KVA + 3 * QB
SPLITS = (QA, QA + KVA, QA + 2 * KVA, QA + 2 * KVA + QB, QA + 2 * KVA + 2 * QB)
D_MIX = QA + QB
N_MOD = 9

kernel_name = "hybrid_chunk_causal_swa_sink_relbias_macaron"


def rms_norm(x, g):
    x32 = x.astype(jnp.float32)
    y = x32 * lax.rsqrt(jnp.mean(x32 * x32, axis=-1, keepdims=True) + EPS)
    return y.astype(x.dtype) * g


def swiglu(h, w_gate, w_up, w_down):
    return (jax.nn.silu(h @ w_gate) * (h @ w_up)) @ w_down


def chunk_band(t, n_back):
    b, s, h, d = t.shape
    n = s // CHUNK
    tc = t.reshape(b, n, CHUNK, h, d)
    tp = jnp.pad(tc, ((0, 0), (n_back, 0), (0, 0), (0, 0), (0, 0)))
    return jnp.concatenate([tp[:, j:j + n] for j in range(n_back + 1)], axis=2)


def band_distance(n_back):
    i = jnp.arange(CHUNK, dtype=jnp.int32)[:, None]
    j = jnp.arange((n_back + 1) * CHUNK, dtype=jnp.int32)[None, :]
    return n_back * CHUNK + i - j


def band_valid(n_chunks, n_back):
    kc = (jnp.arange(n_chunks, dtype=jnp.int32)[:, None] - n_back
          + jnp.arange((n_back + 1) * CHUNK, dtype=jnp.int32)[None, :] // CHUNK)
    return kc >= 0


def band_attention(q, k_band, v_band, bias, valid, sink):
    scale = q.shape[-1] ** -0.5
    s = jnp.einsum('bnqhgd,bnshd->bnhgqs', q, k_band).astype(jnp.float32) * scale
    s = jnp.where(valid[None, :, None, None, None, :], s + bias[None, None], NEG_INF)
    if sink is None:
        p = jax.nn.softmax(s, axis=-1)
    else:
        sk = sink.astype(jnp.float32)[None, None, :, :, None, None]
        m = jnp.maximum(jnp.max(s, axis=-1, keepdims=True), sk)
        e = jnp.exp(s - m)
        p = e / (jnp.sum(e, axis=-1, keepdims=True) + jnp.exp(sk - m))
    return jnp.einsum('bnhgqs,bnshd->bnqhgd', p.astype(v_band.dtype), v_band)


def alibi_slopes(n_heads):
    return jnp.asarray(np.array([2.0 ** (-8.0 * (i + 1) / n_heads) for i in range(n_heads)], dtype=np.float32))


def hybrid_mixer(h, w_in, b_in, sinks_a, rel_bias_b, g_grp_a, g_grp_b, w_out, b_out):
    b, s, _ = h.shape
    n = s // CHUNK
    proj = h @ w_in + b_in
    qa, ka, va, qb, kb, vb = jnp.split(proj, SPLITS, axis=-1)

    qa = qa.reshape(b, n, CHUNK, KV_A, G_A, HEAD_DIM)
    ka_band = chunk_band(ka.reshape(b, s, KV_A, HEAD_DIM), BACK_A)
    va_band = chunk_band(va.reshape(b, s, KV_A, HEAD_DIM), BACK_A)
    dist_a = jnp.abs(band_distance(BACK_A)).astype(jnp.float32)
    bias_a = (-alibi_slopes(H_A)[:, None, None] * dist_a[None]).reshape(KV_A, G_A, CHUNK, -1)
    oa = band_attention(qa, ka_band, va_band, bias_a, band_valid(n, BACK_A),
                        sinks_a.reshape(KV_A, G_A)).reshape(b, s, QA)

    qb = qb.reshape(b, n, CHUNK, H_B, 1, HEAD_DIM)
    kb_band = chunk_band(kb.reshape(b, s, H_B, HEAD_DIM), BACK_B)
    vb_band = chunk_band(vb.reshape(b, s, H_B, HEAD_DIM), BACK_B)
    rel_idx = jnp.clip(band_distance(BACK_B), -REL_CLIP, REL_CLIP) + REL_CLIP
    bias_b = rel_bias_b.astype(jnp.float32)[:, rel_idx][:, None]
    ob = band_attention(qb, kb_band, vb_band, bias_b, band_valid(n, BACK_B), None).reshape(b, s, QB)

    y = jnp.concatenate([rms_norm(oa, g_grp_a), rms_norm(ob, g_grp_b)], axis=-1)
    return y @ w_out + b_out


def sandwich(x, y_fn, g_pre, g_post, shift, scale, gate, weight):
    h = rms_norm(x, g_pre) * (1.0 + scale[:, None, :]) + shift[:, None, :]
    y = rms_norm(y_fn(h), g_post)
    return x + weight * gate[:, None, :] * y


def setup_inputs(seed: int = 0) -> dict:
    key = jax.random.key(seed)
    ks = jax.random.split(key, 32)
    f32 = jnp.float32

    def w(k, shape, fan_in, mult=1.0):
        return jax.random.normal(k, shape, f32) * (mult * fan_in ** -0.5)

    def gain(k, n):
        return 1.0 + 0.05 * jax.random.normal(k, (DEPTH, n), f32)

    def bias(k, n):
        return 0.02 * jax.random.normal(k, (DEPTH, n), f32)

    return {
        "x": jax.random.normal(ks[0], (BATCH, SEQ, D_MODEL), f32),
        "c": jax.random.normal(ks[1], (BATCH, D_MODEL), f32),
        "w_ada": w(ks[2], (DEPTH, D_MODEL, N_MOD * D_MODEL), D_MODEL, 0.3),
        "b_ada": bias(ks[3], N_MOD * D_MODEL),
        "g_pre_ffn1": gain(ks[4], D_MODEL),
        "w_gate1": w(ks[5], (DEPTH, D_MODEL, D_FF), D_MODEL),
        "w_up1": w(ks[6], (DEPTH, D_MODEL, D_FF), D_MODEL),
        "w_down1": w(ks[7], (DEPTH, D_FF, D_MODEL), D_FF),
        "g_post_ffn1": gain(ks[8], D_MODEL),
        "g_pre_mix": gain(ks[9], D_MODEL),
        "w_in": w(ks[10], (DEPTH, D_MODEL, D_IN), D_MODEL),
        "b_in": bias(ks[11], D_IN),
        "sinks_a": jax.random.normal(ks[12], (DEPTH, H_A), f32),
        "rel_bias_b": 0.5 * jax.random.normal(ks[13], (DEPTH, H_B, N_REL), f32),
        "g_grp_a": gain(ks[14], QA),
        "g_grp_b": gain(ks[15], QB),
        "w_out": w(ks[16], (DEPTH, D_MIX, D_MODEL), D_MIX),
        "b_out": bias(ks[17], D_MODEL),
        "g_post_mix": gain(ks[18], D_MODEL),
        "g_pre_ffn2": gain(ks[19], D_MODEL),
        "w_gate2": w(ks[20], (DEPTH, D_MODEL, D_FF), D_MODEL),
        "w_up2": w(ks[21], (DEPTH, D_MODEL, D_FF), D_MODEL),
        "w_down2": w(ks[22], (DEPTH, D_FF, D_MODEL), D_FF),
        "g_post_ffn2": gain(ks[23], D_MODEL),
    }


def reference(x, c, w_ada, b_ada, g_pre_ffn1, w_gate1, w_up1, w_down1, g_post_ffn1,
              g_pre_mix, w_in, b_in, sinks_a, rel_bias_b, g_grp_a, g_grp_b, w_out, b_out,
              g_post_mix, g_pre_ffn2, w_gate2, w_up2, w_down2, g_post_ffn2):
    bsz = c.shape[0]
    for l in range(DEPTH):
        mod = (jax.nn.silu(c) @ w_ada[l] + b_ada[l]).reshape(bsz, N_MOD, D_MODEL)
        x = sandwich(x, lambda h: swiglu(h, w_gate1[l], w_up1[l], w_down1[l]),
                     g_pre_ffn1[l], g_post_ffn1[l], mod[:, 0], mod[:, 1], mod[:, 2], 0.5)
        x = sandwich(x, lambda h: hybrid_mixer(h, w_in[l], b_in[l], sinks_a[l], rel_bias_b[l],
                                               g_grp_a[l], g_grp_b[l], w_out[l], b_out[l]),
                     g_pre_mix[l], g_post_mix[l], mod[:, 3], mod[:, 4], mod[:, 5], 1.0)
        x = sandwich(x, lambda h: swiglu(h, w_gate2[l], w_up2[l], w_down2[l]),
                     g_pre_ffn2[l], g_post_ffn2[l], mod[:, 6], mod[:, 7], mod[:, 8], 0.5)
    return x
```

```python
import numpy as np
from contextlib import ExitStack
import concourse.bass as bass
import concourse.mybir as mybir
from concourse.bass_utils import run_bass_kernel_spmd

F32 = mybir.dt.float32
BF16 = mybir.dt.bfloat16
AF = mybir.ActivationFunctionType
ALU = mybir.AluOpType

D = 1024
SEQ = 2048
DFF = 2816
NJ = DFF // 128
TPH = 8
NSLOT = 3
EPS = 1e-6
NEGM = -30000.0
ENGS = ('pe', 'act', 'dve', 'pool', 'sp')


class Sched:
    def __init__(self):
        self.ops = {e: [] for e in ENGS}
        self.cnt = {}
        self.last_w = {}
        self.readers = {}
        self.known = {e: {} for e in ENGS}
        self.reg_prev = {}
        self.reg_cur = {}

    def phase(self, region):
        prev = self.reg_prev.setdefault(region, {})
        cur = self.reg_cur.setdefault(region, {})
        if cur:
            self.reg_prev[region] = dict(cur)
            self.reg_cur[region] = {}

    def _need(self, eng, ev, needs):
        key, val = ev
        if self.known[eng].get(key, 0) >= val:
            return
        if needs.get(key, 0) < val:
            needs[key] = val

    def op(self, eng, fn, reads=(), writes=(), dma=None, regions=()):
        needs = {}
        own = ('E', eng)
        for r in reads:
            ev = self.last_w.get(r)
            if ev is not None:
                if eng == 'pe' and ev[0] == own:
                    continue
                self._need(eng, ev, needs)
        strict = eng != 'pe'
        for w in writes:
            ev = self.last_w.get(w)
            if ev is not None and (strict or ev[0] != own):
                self._need(eng, ev, needs)
            for ev in self.readers.get(w, ()):
                if strict or ev[0] != own:
                    self._need(eng, ev, needs)
        for rg in regions:
            for k, v in self.reg_prev.get(rg, {}).items():
                if k != own:
                    self._need(eng, (k, v), needs)
        for k, v in needs.items():
            self.known[eng][k] = v
        if dma is None:
            key, inc = own, 1
        else:
            key, inc = ('D', dma), 16
        val = self.cnt.get(key, 0) + inc
        self.cnt[key] = val
        ev = (key, val)
        for w in writes:
            self.last_w[w] = ev
            self.readers[w] = []
        for r in reads:
            lst = self.readers.setdefault(r, [])
            lst[:] = [e for e in lst if e[0] != key]
            lst.append(ev)
        for rg in regions:
            cur = self.reg_cur.setdefault(rg, {})
            cur[key] = max(cur.get(key, 0), val)
        self.ops[eng].append((sorted(needs.items(), key=str), fn, key, inc))
        return ev


def _win_perm():
    QA0, KA0, VA0, QB0, KB0, VB0 = 0, 512, 640, 768, 1280, 1792
    idx = []
    for g in range(4):
        for kv in range(2):
            h = kv * 4 + g
            idx += list(range(QA0 + h * 64, QA0 + (h + 1) * 64))
    idx += list(range(QB0, QB0 + 512)) + list(range(KB0, KB0 + 512)) + list(range(VB0, VB0 + 512))
    idx += list(range(KA0, KA0 + 128)) + list(range(VA0, VA0 + 128))
    return np.array(idx, dtype=np.int64)


HORDER = [0, 1, 2, 3, 4, 5, 6, 7]


def _const_tables():
    s = np.arange(128)[:, None]
    q = np.arange(128)[None, :]
    qc = (q >= 64).astype(np.int64)
    sc = (s >= 64).astype(np.int64)
    slopes = np.array([2.0 ** (-8.0 * (i + 1) / 8) for i in range(8)], dtype=np.float64)
    biasA = np.zeros((128, 2, 8, 128), dtype=np.float32)
    for ko, koff in enumerate((-1, 0)):
        dist = np.abs(q - s - koff * 128).astype(np.float64)
        dchunk = qc - (2 * koff + sc)
        valid = (dchunk >= 0) & (dchunk <= 2)
        for h in range(8):
            b = -slopes[h] * dist
            biasA[:, ko, h, :] = np.where(valid, b, NEGM).astype(np.float32)
    maskB = np.zeros((128, 4, 128), dtype=np.float32)
    for ty, koff in enumerate((0, -1, -2, -4)):
        dchunk = qc - (2 * koff + sc)
        valid = (dchunk >= 0) & (dchunk <= 8)
        maskB[:, ty, :] = np.where(valid, 0.0, NEGM).astype(np.float32)
    ident = np.eye(128, dtype=np.float32)
    return biasA * 8.0, maskB * 8.0, ident


def build_nc(stage=3, dbg=False):
    nc = bass.Bass("TRN2", target_bir_lowering=False)

    def din(name, shape, dt=F32):
        return nc.dram_tensor(name, list(shape), dt, kind="ExternalInput")

    x_d = din("x", [SEQ, D]).ap()
    c_d = din("c", [8, 128]).ap()
    wada_d = din("w_ada", [D, 9 * D]).ap()
    bada_d = din("b_ada", [72, 128]).ap()
    gpre_d = din("g_pre", [24, 128]).ap()
    gpost_d = din("g_post", [24, 128]).ap()
    wg_d = [din("w_gate1", [D, DFF]).ap(), din("w_gate2", [D, DFF]).ap()]
    wu_d = [din("w_up1", [D, DFF]).ap(), din("w_up2", [D, DFF]).ap()]
    wd_d = [din("w_down1", [DFF, D]).ap(), din("w_down2", [DFF, D]).ap()]
    win_d = din("w_in", [D, 2304]).ap()
    bqk_d = din("b_qk", [13, 128]).ap()
    brow_d = din("b_row", [1, 1664]).ap()
    sinks_d = din("sinks", [1, 8]).ap()
    rbT_d = din("rbT", [128, 4096]).ap()
    ggrp_d = din("g_grp", [8, 128]).ap()
    wout_d = din("w_out", [D, D]).ap()
    biasA_d = din("biasA", [128, 2048]).ap()
    maskB_d = din("maskB", [128, 512]).ap()
    ident_d = din("ident", [128, 128]).ap()
    out_d = nc.dram_tensor("out", [SEQ, D], F32, kind="ExternalOutput").ap()

    es = ExitStack()

    def sb(name, shape, dt):
        return es.enter_context(nc.sbuf_tensor(name, list(shape), dt))

    X = sb("X", [128, TPH, D], F32)
    HT = sb("HT", [128, 8, 1024], BF16)
    R1 = sb("R1", [128, 11264], F32)
    R2 = sb("R2", [128, 11264], F32)
    RING = sb("RING", [128, NSLOT, 8, 512], BF16)
    KAP = sb("KAP", [128, 512], BF16)
    KBP = sb("KBP", [128, 4, 512], BF16)
    VAP = sb("VAP", [128, 4, 130], BF16)
    VBP = sb("VBP", [128, 4, 520], BF16)
    GG = sb("GG", [128, 1024], F32)
    EA = sb("EA", [128, 2, 8, 128], BF16)
    EB = sb("EB", [128, 4, 8, 128], BF16)
    XH = sb("XH", [128, 2, 1024], F32)
    SG = sb("SG", [128, 2, 512], F32)
    COLA = sb("COLA", [128, 80], F32)
    COLB = sb("COLB", [128, 69], F32)
    MODT = sb("MODT", [128, 72], F32)
    GMOD = sb("GMOD", [128, 3, 8], F32)
    IDENT = sb("IDENT", [128, 128], F32)
    IDENTB = sb("IDENTB", [128, 128], BF16)
    YB = sb("YB", [128, 1024], BF16)
    DG = sb("DG", [128, 2, 128], F32)
    SC = sb("SC", [128, 8], BF16)
    EXPS = sb("EXPS", [128, 8], F32)
    SINK = sb("SINK", [128, 8], F32)
    ST = sb("ST", [128, 128], F32)
    EPSC = sb("EPSC", [128, 1], F32)
    BHL = sb("BHL", [128, 1664], BF16)
    ONES = sb("ONES", [128, 128], BF16)

    PSALL = es.enter_context(nc.psum_tensor("psall", [128, 8, 512], F32))
    PS = [PSALL[:, i, :] for i in range(8)]
    PSB = PSALL[:].bitcast(BF16)

    JNK = SG[:].bitcast(BF16)[:, 0, :]
    VECA = R2[0:80, 2048:2176]
    VECB = R2[0:69, 2304:2432]
    BROW0 = R2[0:1, 0:1664]
    XHB = XH[:].bitcast(BF16)
    R1b = R1[:].bitcast(BF16)
    R2b = R2[:].bitcast(BF16)
    AT = R1b.rearrange("p (j t) -> p j t", t=1024)
    WD = R2b.rearrange("p (j t) -> p j t", t=1024)
    QAT = R2b[:, 0:4096].rearrange("p (g t) -> p g t", t=1024)
    QBT = R2b[:, 4096:8192].rearrange("p (g t) -> p g t", t=1024)
    KBT = R2b[:, 8192:12288].rearrange("p (g t) -> p g t", t=1024)
    KAT = R2b[:, 12288:13312]
    VA = R2b[:, 13312:14352].rearrange("p (t e) -> p t e", e=130)
    VB = R2b[:, 14352:18512].rearrange("p (t e) -> p t e", e=520)
    NPT = 6
    PT = [R1b[:, i * 512:(i + 1) * 512] for i in range(NPT)]
    QAP = [R1b[:, 4096 + i * 1024:4096 + (i + 1) * 1024].rearrange("p (a g q) -> p a g q", a=2, g=4) for i in range(2)]
    QBP = [R1b[:, 6144 + i * 1024:6144 + (i + 1) * 1024].rearrange("p (h q) -> p h q", q=128) for i in range(2)]
    SCRF = R1[:, 4096:8192]
    SCRM = R1[:, 8192:8704]
    SCRA = R1[:, 8704:10752]

    S = Sched()
    stcol = [0]

    def newst(n=1):
        c = stcol[0]
        if c + n > 128:
            c = 0
        stcol[0] = c + n
        return c

    ring_items = []
    ring_state = {'issued': 0}

    def ring_issue_upto(n):
        while ring_state['issued'] < min(n, len(ring_items)):
            i = ring_state['issued']
            slot = i % NSLOT
            for part, (src, c0, cw) in enumerate(ring_items[i]):
                S.op('pool',
                     (lambda e, slot=slot, src=src, c0=c0, cw=cw:
                      e.dma_start(out=RING[:, slot, :, c0:c0 + cw], in_=src)),
                     writes=[("ring", slot, part)], dma=("ring", slot, part))
            ring_state['issued'] = i + 1

    def ring_keys(i):
        assert ring_state['issued'] > i, (i, ring_state)
        return [("ring", i % NSLOT, p) for p in range(len(ring_items[i]))]

    def ring_release(i):
        ring_issue_upto(i + NSLOT + 1)

    def wsrc(w_ap, c0, cw):
        return w_ap.rearrange("(k p) f -> p k f", p=128)[:, :, c0:c0 + cw]

    def plan_half(hf):
        plan = {}

        def add(tag, parts):
            plan[tag] = len(ring_items)
            ring_items.append(parts)

        def ada(i):
            if hf == 0:
                for hh in range(2):
                    add(("ada", i, hh), [(wsrc(wada_d, i * 1024 + hh * 512, 512), 0, 512)])

        ada(0); ada(1)
        for f in range(2 if stage >= 3 else 1):
            if f == 1:
                ada(6); ada(7)
            for g in range(11):
                add(("gu", f, g), [(wsrc(wg_d[f], g * 256, 256), 0, 256),
                                   (wsrc(wu_d[f], g * 256, 256), 256, 256)])
            ada(2 if f == 0 else 8)
            if f == 0 and stage >= 2:
                ada(3); ada(4)
                for i in range(4):
                    add(("win", i), [(wsrc(win_d, i * 512, 512), 0, 512)])
                add(("win", 4), [(wsrc(win_d, 2048, 256), 0, 256)])
                ada(5)
                for i in range(2):
                    add(("wout", i), [(wsrc(wout_d, i * 512, 512), 0, 512)])
        return plan

    plans = [plan_half(0), plan_half(1)]

    def ps_key(i):
        return ("ps", i)

    def dma_sp(out, in_, writes, reads=(), ch=None, regions=()):
        S.op('sp', lambda e: e.dma_start(out=out, in_=in_), reads=reads, writes=writes,
             dma=ch, regions=regions)

    def setup():
        dma_sp(IDENT[:], ident_d, [("ident",)], ch=("c", 0))
        dma_sp(VECA[0:72, :], bada_d, [("veca", 0)], ch=("c", 1), regions=("R2",))
        dma_sp(VECA[72:80, :], c_d, [("veca", 1)], ch=("c", 2), regions=("R2",))
        dma_sp(VECB[0:24, :], gpre_d, [("vecb", 0)], ch=("c", 3), regions=("R2",))
        dma_sp(VECB[24:48, :], gpost_d, [("vecb", 1)], ch=("c", 4), regions=("R2",))
        dma_sp(VECB[48:61, :], bqk_d, [("vecb", 2)], ch=("c", 5), regions=("R2",))
        dma_sp(VECB[61:69, :], ggrp_d, [("vecb", 3)], ch=("c", 6), regions=("R2",))
        dma_sp(BROW0, brow_d, [("brow", 0)], ch=("c", 7), regions=("R2",))
        dma_sp(SINK[:], sinks_d.partition_broadcast(128), [("sink",)], ch=("c", 8))
        for t in range(TPH):
            dma_sp(X[:, t, :], x_d[t * 128:(t + 1) * 128, :], [("X", t)], ch=("x", t))
        S.op('pe', lambda e: e.transpose(out=PS[0][:, 0:80], in_=VECA, identity=IDENT[0:80, 0:80]),
             reads=[("veca", 0), ("veca", 1), ("ident",)], writes=[ps_key(0)], regions=("R2",))
        S.op('dve', lambda e: e.tensor_copy(out=COLA[:], in_=PS[0][:, 0:80]),
             reads=[ps_key(0)], writes=[("cola",)])
        S.op('pe', lambda e: e.transpose(out=PS[1][:, 0:69], in_=VECB, identity=IDENT[0:69, 0:69]),
             reads=[("vecb", i) for i in range(4)] + [("ident",)], writes=[ps_key(1)], regions=("R2",))
        S.op('dve', lambda e: e.tensor_copy(out=COLB[:], in_=PS[1][:, 0:69]),
             reads=[ps_key(1)], writes=[("colb",)])
        S.op('dve', lambda e: e.tensor_copy(out=IDENTB[:], in_=IDENT[:]), reads=[("ident",)], writes=[("identb",)])
        S.op('act', lambda e: e.activation(out=SC[:], in_=COLA[:, 72:80], func=AF.Silu),
             reads=[("cola",)], writes=[("sc",)])
        S.op('dve', lambda e: e.memset(ONES[:], 0.0), writes=[("ones",)])
        S.op('dve', lambda e: e.memset(ONES[0:2, :], 1.0), writes=[("ones",)])
        S.op('dve', lambda e: e.memset(BHL[:], 0.0), writes=[("bhi",), ("blo",)])
        S.op('dve', lambda e: e.memset(EPSC[:], EPS), writes=[("epsc",)])
        LO0 = R2[:].bitcast(BF16)[0:1, 8192:8192 + 1664]
        S.op('dve', lambda e: e.tensor_copy(out=BHL[0:1, :], in_=BROW0), reads=[("brow", 0)], writes=[("bhi",)],
             regions=("R2",))
        S.op('dve', lambda e: e.tensor_tensor(out=LO0, in0=BROW0, in1=BHL[0:1, :], op=ALU.subtract),
             reads=[("brow", 0), ("bhi",)], writes=[("lo0",)], regions=("R2",))
        dma_sp(BHL[1:2, :], LO0, [("blo",)], reads=[("lo0",)], ch=("c", 19), regions=("R2",))
        S.op('act', lambda e: e.activation(out=EXPS[:], in_=SINK[:], func=AF.Exp),
             reads=[("sink",)], writes=[("exps",)])

    def setup_tables():
        xk = []
        dma_sp(SCRA, biasA_d, [("scra",)], reads=xk, ch=("c", 9), regions=("R1",))
        dma_sp(SCRM, maskB_d, [("scrm",)], reads=xk, ch=("c", 10), regions=("R1",))
        dma_sp(SCRF, rbT_d, [("scrf",)], reads=xk, ch=("c", 11), regions=("R1",))

    def setup_tables_ops():
        S.op('dve', lambda e: e.tensor_copy(out=EA[:].rearrange("p a h q -> p (a h q)"),
                                            in_=SCRA),
             reads=[("scra",)], writes=[("ea",)], regions=("R1",))
        for ty in range(4):
            S.op('dve', (lambda e, ty=ty: e.scalar_tensor_tensor(
                out=EB[:, ty, :, :], in0=SCRF[:, ty * 1024:(ty + 1) * 1024].rearrange("p (h q) -> p h q", q=128), scalar=8.0,
                in1=SCRM[:, ty * 128:(ty + 1) * 128].unsqueeze(1).broadcast_to([128, 8, 128]),
                op0=ALU.mult, op1=ALU.add)),
                reads=[("scrm",), ("scrf",)], writes=[("eb",)], regions=("R1",))


    def ada_item(hf, i, hh):
        idx = plans[hf][("ada", i, hh)]
        slot = idx % NSLOT
        bank = 7

        def fn(e):
            ins = None
            for cc in range(4):
                for k in range(8):
                    ins = e.matmul(PS[bank][:, cc:cc + 1], lhsT=RING[:, slot, k, cc * 128:(cc + 1) * 128],
                                   rhs=SC[:, k:k + 1], start=(k == 0), stop=(k == 7))
            return ins
        S.op('pe', fn, reads=ring_keys(idx) + [("sc",)], writes=[ps_key(bank)])
        ring_release(idx)
        c0 = i * 8 + hh * 4
        S.op('dve', lambda e: e.tensor_tensor(out=MODT[:, c0:c0 + 4], in0=PS[bank][:, 0:4],
                                              in1=COLA[:, c0:c0 + 4], op=ALU.add),
             reads=[ps_key(bank), ("cola",)], writes=[("modt", i, hh)])

    def ada_mod(hf, i):
        if hf != 0:
            return
        for hh in range(2):
            ada_item(hf, i, hh)

    def make_gmod(s):
        S.op('dve', lambda e: e.scalar_tensor_tensor(out=GMOD[:, s, :], in0=MODT[:, (3 * s + 1) * 8:(3 * s + 1) * 8 + 8],
                                                     scalar=1.0, in1=COLB[:, 8 * s:8 * s + 8],
                                                     op0=ALU.add, op1=ALU.mult),
             reads=[("modt", 3 * s + 1, 0), ("modt", 3 * s + 1, 1), ("colb",)], writes=[("gmod", s)])

    def make_gg(s, wgt):
        gi = 3 * s + 2
        for kk in range(8):
            b = kk % 2
            S.op('dve', (lambda e, kk=kk, b=b: e.tensor_scalar(
                out=DG[:, b, :], in0=IDENT[:], scalar1=COLB[:, 24 + 8 * s + kk:24 + 8 * s + kk + 1],
                scalar2=None, op0=ALU.mult)),
                reads=[("ident",), ("colb",)], writes=[("dg", b)])
            bank = 6 + (kk // 4)
            S.op('pe', (lambda e, kk=kk, b=b, bank=bank: e.matmul(
                PS[bank][:, (kk % 4) * 128:(kk % 4 + 1) * 128],
                lhsT=MODT[:, gi * 8 + kk:gi * 8 + kk + 1].broadcast_to([128, 128]),
                rhs=DG[:, b, :], start=True, stop=True)),
                reads=[("dg", b), ("modt", gi, 0), ("modt", gi, 1)], writes=[ps_key(bank)])
        for hh in range(2):
            S.op('act', (lambda e, hh=hh: e.mul(out=GG[:, hh * 512:(hh + 1) * 512], in_=PS[6 + hh][:], mul=wgt)),
                 reads=[ps_key(6 + hh)], writes=[("gg", hh)])

    def prenorm(s):
        c = newst(8)
        for t in range(TPH):
            S.op('act', (lambda e, t=t: e.activation(out=JNK[:], in_=X[:, t, :], func=AF.Square,
                                                     accum_out=ST[:, c + t:c + t + 1])),
                 reads=[("X", t)], writes=[("sg", 0), ("st", c + t)])
        c2 = newst(8)
        S.op('act', lambda e: e.activation(out=ST[:, c2:c2 + 8], in_=ST[:, c:c + 8], func=AF.Sqrt,
                                           bias=EPSC[:, 0:1], scale=1.0 / D),
             reads=[("st", c + t) for t in range(8)] + [("epsc",)], writes=[("st", c2 + t) for t in range(8)])
        c3 = newst(8)
        S.op('dve', lambda e: e.reciprocal(out=ST[:, c3:c3 + 8], in_=ST[:, c2:c2 + 8]),
             reads=[("st", c2 + t) for t in range(8)], writes=[("st", c3 + t) for t in range(8)])
        for tb in range(2):
            for tt in range(4):
                t = tb * 4 + tt
                b = t % 2
                S.op('dve', (lambda e, t=t, b=b: e.tensor_scalar(
                    out=XHB[:, b, 0:1024], in0=X[:, t, :], scalar1=ST[:, c3 + t:c3 + t + 1], scalar2=None,
                    op0=ALU.mult)),
                    reads=[("X", t), ("st", c3 + t)], writes=[("xh", b)])

                def fn(e, tt=tt, b=b):
                    ins = None
                    for k in range(8):
                        ins = e.transpose(out=PSB[:, k, tt * 128:(tt + 1) * 128],
                                          in_=XHB[:, b, k * 128:(k + 1) * 128], identity=IDENTB[:])
                    return ins
                S.op('pe', fn, reads=[("xh", b), ("identb",)], writes=[ps_key(k) for k in range(8)])
            for k in range(8):
                wr = [("HT", tb * 4 + tt, k) for tt in range(4)]
                if k % 2 == 0:
                    S.op('act', (lambda e, k=k, tb=tb: e.activation(
                        out=HT[:, k, tb * 512:(tb + 1) * 512], in_=PSB[:, k, 0:512], func=AF.Identity,
                        bias=MODT[:, 3 * s * 8 + k:3 * s * 8 + k + 1], scale=GMOD[:, s, k:k + 1])),
                        reads=[ps_key(k), ("gmod", s), ("modt", 3 * s, 0), ("modt", 3 * s, 1)], writes=wr)
                else:
                    S.op('dve', (lambda e, k=k, tb=tb: e.tensor_scalar(
                        out=HT[:, k, tb * 512:(tb + 1) * 512], in0=PSB[:, k, 0:512], scalar1=GMOD[:, s, k:k + 1],
                        scalar2=MODT[:, 3 * s * 8 + k:3 * s * 8 + k + 1], op0=ALU.mult, op1=ALU.add)),
                        reads=[ps_key(k), ("gmod", s), ("modt", 3 * s, 0), ("modt", 3 * s, 1)], writes=wr)

    def prenorm_tile(s, t, bank, lnexp=False, dve_only=False):
        prenorm_front(t, lnexp)
        prenorm_back(s, t, bank, dve_only)

    def prenorm_seq(s, nbanks=4):
        for t in range(TPH):
            prenorm_front(t)
            if t >= 1:
                prenorm_back(s, t - 1, (t - 1) % nbanks)
        prenorm_back(s, TPH - 1, (TPH - 1) % nbanks)

    def prenorm_front(t, lnexp=False):
        c = newst(1)
        S.op('act', lambda e: e.activation(out=JNK[:], in_=X[:, t, :], func=AF.Square, accum_out=ST[:, c:c + 1]),
             reads=[("X", t)], writes=[("sg", 0), ("st", c)])
        c2 = newst(1)
        c3 = newst(1)
        if lnexp:
            S.op('act', lambda e: e.activation(out=ST[:, c2:c2 + 1], in_=ST[:, c:c + 1], func=AF.Ln,
                                               bias=EPSC[:, 0:1], scale=1.0 / D),
                 reads=[("st", c), ("epsc",)], writes=[("st", c2)])
            S.op('act', lambda e: e.activation(out=ST[:, c3:c3 + 1], in_=ST[:, c2:c2 + 1], func=AF.Exp, scale=-0.5),
                 reads=[("st", c2)], writes=[("st", c3)])
        else:
            S.op('act', lambda e: e.activation(out=ST[:, c2:c2 + 1], in_=ST[:, c:c + 1], func=AF.Sqrt,
                                               bias=EPSC[:, 0:1], scale=1.0 / D),
                 reads=[("st", c), ("epsc",)], writes=[("st", c2)])
            S.op('dve', lambda e: e.reciprocal(out=ST[:, c3:c3 + 1], in_=ST[:, c2:c2 + 1]),
                 reads=[("st", c2)], writes=[("st", c3)])
        b = t % 2
        S.op('dve', lambda e: e.tensor_scalar(out=XHB[:, b, 0:1024], in0=X[:, t, :], scalar1=ST[:, c3:c3 + 1],
                                              scalar2=None, op0=ALU.mult),
             reads=[("X", t), ("st", c3)], writes=[("xh", b)])

    def prenorm_back(s, t, bank, dve_only=False):
        b = t % 2

        def fn(e):
            ins = None
            for k in range(8):
                ins = e.transpose(out=PSB[:, bank, k * 128:(k + 1) * 128],
                                  in_=XHB[:, b, k * 128:(k + 1) * 128], identity=IDENTB[:])
            return ins
        S.op('pe', fn, reads=[("xh", b), ("identb",)], writes=[ps_key(bank)])
        for k in range(8):
            rd = [ps_key(bank), ("gmod", s), ("modt", 3 * s, 0), ("modt", 3 * s, 1)]
            if t % 2 == 0 and not dve_only:
                S.op('act', (lambda e, k=k: e.activation(
                    out=HT[:, k, t * 128:(t + 1) * 128], in_=PSB[:, bank, k * 128:(k + 1) * 128], func=AF.Identity,
                    bias=MODT[:, 3 * s * 8 + k:3 * s * 8 + k + 1], scale=GMOD[:, s, k:k + 1])),
                    reads=rd, writes=[("HT", t, k)])
            else:
                S.op('dve', (lambda e, k=k: e.tensor_scalar(
                    out=HT[:, k, t * 128:(t + 1) * 128], in0=PSB[:, bank, k * 128:(k + 1) * 128],
                    scalar1=GMOD[:, s, k:k + 1], scalar2=MODT[:, 3 * s * 8 + k:3 * s * 8 + k + 1],
                    op0=ALU.mult, op1=ALU.add)),
                    reads=rd, writes=[("HT", t, k)])

    def postnorm_residual(t, b0, b1, final, hf, lnexp=False, after=None):
        assert b1 == b0 + 1
        c = newst(1)
        S.op('act', lambda e: e.activation(out=JNK.rearrange("p (a b) -> p a b", a=2), in_=PSALL[:, b0:b0 + 2, :],
                                           func=AF.Square, accum_out=ST[:, c:c + 1]),
             reads=[ps_key(b0), ps_key(b1)], writes=[("sg", 0), ("st", c)])
        c2 = newst(1)
        c3 = newst(1)
        if lnexp:
            S.op('act', lambda e: e.activation(out=ST[:, c2:c2 + 1], in_=ST[:, c:c + 1], func=AF.Ln,
                                               bias=EPSC[:, 0:1], scale=1.0 / D),
                 reads=[("st", c), ("epsc",)], writes=[("st", c2)])
            S.op('act', lambda e: e.activation(out=ST[:, c3:c3 + 1], in_=ST[:, c2:c2 + 1], func=AF.Exp, scale=-0.5),
                 reads=[("st", c2)], writes=[("st", c3)])
        else:
            S.op('act', lambda e: e.activation(out=ST[:, c2:c2 + 1], in_=ST[:, c:c + 1], func=AF.Sqrt,
                                               bias=EPSC[:, 0:1], scale=1.0 / D),
                 reads=[("st", c), ("epsc",)], writes=[("st", c2)])
            S.op('dve', lambda e: e.reciprocal(out=ST[:, c3:c3 + 1], in_=ST[:, c2:c2 + 1]),
                 reads=[("st", c2)], writes=[("st", c3)])
        xb = t % 2
        S.op('dve', lambda e: e.scalar_tensor_tensor(
            out=XH[:, xb, :].rearrange("p (a b) -> p a b", a=2), in0=PSALL[:, b0:b0 + 2, :], scalar=ST[:, c3:c3 + 1],
            in1=GG[:].rearrange("p (a b) -> p a b", a=2), op0=ALU.mult, op1=ALU.mult),
            reads=[ps_key(b0), ps_key(b1), ("st", c3), ("gg", 0), ("gg", 1)], writes=[("xh", xb)])
        S.op('dve', lambda e: e.tensor_tensor(out=X[:, t, :], in0=X[:, t, :], in1=XH[:, xb, :], op=ALU.add),
             reads=[("X", t), ("xh", xb)], writes=[("X", t)])
        if final:
            r0 = (hf * TPH + t) * 128
            dma_sp(out_d[r0:r0 + 128, :], X[:, t, :], writes=[], reads=[("X", t)], ch=("o", t))
            if hf == 0:
                r1 = (TPH + t) * 128
                dma_sp(X[:, t, :], x_d[r1:r1 + 128, :], [("X", t)], ch=("x", t))
        if after is not None:
            after(t)

    PEND = {}

    def ffn(hf, f, s, final):
        plan = plans[hf]
        S.phase("R1")
        S.phase("R2")
        wdv = wd_d[f].rearrange("(j p) c -> p j c", p=128)
        for part in range(2):
            S.op('pool', (lambda e, part=part: e.dma_start(out=WD[:, part * 11:(part + 1) * 11, :],
                                                           in_=wdv[:, part * 11:(part + 1) * 11, :])),
                 writes=[("wd", part)], dma=("wd", part), regions=("R2",))
        if hf == 1:
            make_gg(s, 0.5)
        ui = 0
        for g in range(11):
            idx = plan[("gu", f, g)]
            slot = idx % NSLOT
            order = [(jj, tb) for jj in range(2) for tb in range(2)]
            hook = PEND.pop("back7", None) if (g == 0 and hf == 1 and f == 0) else None
            if hook is not None:
                order = [(0, 0), (1, 0), (0, 1), (1, 1)]
            for oi, (jj, tb) in enumerate(order):
                if hook is not None and oi == 2:
                    hook()
                j = g * 2 + jj
                if True:
                    bG = (ui % 4) * 2
                    bU = bG + 1
                    sgi = ui % 2
                    ui += 1

                    def fn(e, slot=slot, jj=jj, tb=tb, bG=bG, bU=bU):
                        ins = None
                        for k in range(8):
                            ins = e.matmul(PS[bG][:], lhsT=RING[:, slot, k, jj * 128:(jj + 1) * 128],
                                           rhs=HT[:, k, tb * 512:(tb + 1) * 512], start=(k == 0), stop=(k == 7))
                        for k in range(8):
                            ins = e.matmul(PS[bU][:], lhsT=RING[:, slot, k, 256 + jj * 128:256 + (jj + 1) * 128],
                                           rhs=HT[:, k, tb * 512:(tb + 1) * 512], start=(k == 0), stop=(k == 7))
                        return ins
                    S.op('pe', fn, reads=ring_keys(idx) + [("HT", tb * 4 + tt, k) for tt in range(4) for k in range(8)],
                         writes=[ps_key(bG), ps_key(bU)])
                    S.op('act', (lambda e, bG=bG, sgi=sgi: e.activation(out=SG[:, sgi, :], in_=PS[bG][:], func=AF.Silu)),
                         reads=[ps_key(bG)], writes=[("sg", sgi)])
                    S.op('dve', (lambda e, bU=bU, sgi=sgi, j=j, tb=tb: e.tensor_tensor(
                        out=AT[:, j, tb * 512:(tb + 1) * 512], in0=SG[:, sgi, :], in1=PS[bU][:], op=ALU.mult)),
                        reads=[("sg", sgi), ps_key(bU)], writes=[("AT", j, tb)], regions=("R1",))
            ring_release(idx)
        ada_mod(hf, 3 * s + 2)
        if hf == 0:
            make_gg(s, 0.5)
        after = None
        if f == 0 and stage >= 2 and hf == 1:
            nxt = 1
            after = lambda t: prenorm_front(t)
        elif final and hf == 0 and stage >= 3:
            nxt = 0
            after = "lag2"
        for t in range(TPH):
            banks = (0, 1) if t % 2 == 0 else (2, 3)
            for hh in range(2):
                bk = banks[hh]

                def fn(e, t=t, hh=hh, bk=bk):
                    ins = None
                    for j in range(NJ):
                        ins = e.matmul(PS[bk][:], lhsT=AT[:, j, t * 128:(t + 1) * 128],
                                       rhs=WD[:, j, hh * 512:(hh + 1) * 512], start=(j == 0), stop=(j == NJ - 1))
                    return ins
                S.op('pe', fn, reads=[("AT", j, t // 4) for j in range(NJ)] + [("wd", 0), ("wd", 1)],
                     writes=[ps_key(bk)], regions=("R1", "R2"))
            if after == "lag2":
                if t >= 2:
                    prenorm_back(nxt, t - 2, 4 + (t - 2) % 2)
                postnorm_residual(t, banks[0], banks[1], final, hf)
                if t >= 1:
                    prenorm_front(t - 1)
                continue
            postnorm_residual(t, banks[0], banks[1], final, hf, after=after)
            if after is not None and t >= 1:
                prenorm_back(nxt, t - 1, 4 + (t - 1) % 2)
        if after == "lag2":
            prenorm_back(nxt, TPH - 2, 4 + (TPH - 2) % 2)
            prenorm_front(TPH - 1)
            PEND["back7"] = lambda: prenorm_back(0, TPH - 1, 4 + (TPH - 1) % 2)
        elif after is not None:
            prenorm_back(nxt, TPH - 1, 4 + (TPH - 1) % 2)

    def mixer(hf):
        KMIX = 9
        plan = plans[hf]
        s = 1
        S.phase("R1")
        S.phase("R2")
        if hf == 0:
            setup_tables()
        HTk = lambda tb: [("HT", tb * 4 + tt, k) for tt in range(4) for k in range(8)]
        ev_i = [0]

        def evac_T(dst, bank, bcol, wkey):
            i = ev_i[0]
            ev_i[0] += 1
            if i % 2 == 0:
                S.op('act', lambda e: e.activation(out=dst, in_=PS[bank][:], func=AF.Identity,
                                                   bias=COLB[:, bcol:bcol + 1], scale=1.0),
                     reads=[ps_key(bank), ("colb",)], writes=[wkey], regions=("R2",))
            else:
                S.op('dve', lambda e: e.tensor_scalar(out=dst, in0=PS[bank][:], scalar1=COLB[:, bcol:bcol + 1],
                                                      scalar2=None, op0=ALU.add),
                     reads=[ps_key(bank), ("colb",)], writes=[wkey], regions=("R2",))

        bank_i = [0]

        def proj_T(idx, c0, dst3, gi, bcol, name):
            slot = idx % NSLOT
            for tb in range(2):
                bank = bank_i[0] % 4
                bank_i[0] += 1

                def fn(e, tb=tb, bank=bank):
                    ins = None
                    for k in range(8):
                        ins = e.matmul(PS[bank][:], lhsT=RING[:, slot, k, c0:c0 + 128],
                                       rhs=HT[:, k, tb * 512:(tb + 1) * 512], start=(k == 0), stop=(k == 7))
                    return ins
                S.op('pe', fn, reads=ring_keys(idx) + HTk(tb), writes=[ps_key(bank)])
                dst = dst3[:, gi, tb * 512:(tb + 1) * 512] if gi is not None else dst3[:, tb * 512:(tb + 1) * 512]
                evac_T(dst, bank, bcol, (name, gi, tb))

        S.op('dve', lambda e: e.memset(R1b[:, 4096:8192], 0.0), writes=[("qap", 0), ("qap", 1), ("qbp", 0), ("qbp", 1)],
             regions=("R1",))
        S.op('dve', lambda e: e.memset(VA.rearrange("p t (h e) -> p t h e", e=65)[:, :, :, 64:65], 1.0),
             writes=[("vaones",)], regions=("R2",))
        S.op('dve', lambda e: e.memset(VB.rearrange("p t (h e) -> p t h e", e=65)[:, :, :, 64:65], 1.0),
             writes=[("vbones",)], regions=("R2",))

        i0 = plan[("win", 0)]
        for g in range(4):
            proj_T(i0, g * 128, QAT, g, 48 + g, "QAT")
        ring_release(i0)
        i1 = plan[("win", 1)]
        for g in range(4):
            proj_T(i1, g * 128, QBT, g, 52 + g, "QBT")
        ring_release(i1)
        i2 = plan[("win", 2)]
        for g in range(4):
            proj_T(i2, g * 128, KBT, g, 56 + g, "KBT")
        ring_release(i2)
        i3 = plan[("win", 3)]
        slot3 = i3 % NSLOT
        for t in range(TPH):
            bank = 4 + t % 2

            def fn(e, t=t, bank=bank):
                for k in range(8):
                    e.matmul(PS[bank][:], lhsT=HT[:, k, t * 128:(t + 1) * 128], rhs=RING[:, slot3, k, 0:512],
                             start=(k == 0), stop=False)
                return e.matmul(PS[bank][:], lhsT=ONES[:, :], rhs=BHL[:, 0:512], start=False, stop=True)
            S.op('pe', fn, reads=ring_keys(i3) + [("HT", t, k) for k in range(8)] + [("ones",), ("bhi",), ("blo",)],
                 writes=[ps_key(bank)])
            S.op('act', (lambda e, t=t, bank=bank: e.copy(
                out=VB[:, t, :].rearrange("p (h e) -> p h e", e=65)[:, :, 0:64],
                in_=PS[bank][:].rearrange("p (h d) -> p h d", d=64))),
                reads=[ps_key(bank)], writes=[("VB", t)], regions=("R2",))
        ring_release(i3)
        i4 = plan[("win", 4)]
        slot4 = i4 % NSLOT
        proj_T(i4, 0, KAT, None, 60, "KAT")
        for t in range(TPH):
            bank = 4 + t % 2

            def fn(e, t=t, bank=bank):
                for k in range(8):
                    e.matmul(PS[bank][:, 0:128], lhsT=HT[:, k, t * 128:(t + 1) * 128],
                             rhs=RING[:, slot4, k, 128:256], start=(k == 0), stop=False)
                return e.matmul(PS[bank][:, 0:128], lhsT=ONES[:, :], rhs=BHL[:, 512:640], start=False, stop=True)
            S.op('pe', fn, reads=ring_keys(i4) + [("HT", t, k) for k in range(8)] + [("ones",), ("bhi",), ("blo",)],
                 writes=[ps_key(bank)])
            S.op('dve', (lambda e, t=t, bank=bank: e.tensor_copy(
                out=VA[:, t, :].rearrange("p (h e) -> p h e", e=65)[:, :, 0:64],
                in_=PS[bank][:, 0:128].rearrange("p (h d) -> p h d", d=64))),
                reads=[ps_key(bank)], writes=[("VA", t)], regions=("R2",))
        ring_release(i4)

        if hf == 0:
            setup_tables_ops()
        ada_mod(hf, 5)
        make_gg(1, 1.0)

        pti = [0]

        def kv_src(m, koff):
            mk = m + koff
            if mk >= 0:
                ka = KAT[:, mk * 128:(mk + 1) * 128]
                kb = lambda j: KBT[:, j, mk * 128:(mk + 1) * 128]
                va = VA[:, mk, :]
                vb = VB[:, mk, :]
                kk = [("KAT", None, mk // 4)], [("KBT", j, mk // 4) for j in range(4)], [("VA", mk), ("vaones",)], [("VB", mk), ("vbones",)]
            else:
                pk = 4 + mk
                ka = KAP[:, pk * 128:(pk + 1) * 128]
                kb = lambda j: KBP[:, j, pk * 128:(pk + 1) * 128]
                va = VAP[:, pk, :]
                vb = VBP[:, pk, :]
                kk = [("kvp",)], [("kvp",)], [("kvp",)], [("kvp",)]
            return ka, kb, va, vb, kk

        pending = []
        DSKEW = 2
        SBANKS = (0, 1, 5)
        started = set()

        def oslot(i, mtile):
            bank = 2 + i // 7
            st_ = (mtile, bank) not in started
            started.add((mtile, bank))
            return bank, (i % 7) * 65, st_

        def defer(th):
            pending.append(th)

        def drain(keep):
            while len(pending) > keep:
                pending.pop(0)()

        def post_tile(m):
            finalize(m)
            if m >= 1:
                tail_b(m - 1)

        def pads(m):
            qb_ = m % 2
            tl = slice(m * 128, (m + 1) * 128)
            S.op('dve', lambda e: e.tensor_copy(out=QAP[qb_][0:64, 0, :, :], in_=QAT[0:64, :, tl]),
                 reads=[("QAT", g, m // 4) for g in range(4)], writes=[("qap", qb_)], regions=("R1", "R2"))
            S.op('dve', lambda e: e.tensor_copy(out=QAP[qb_][64:128, 1, :, :], in_=QAT[64:128, :, tl]),
                 reads=[("QAT", g, m // 4) for g in range(4)], writes=[("qap", qb_)], regions=("R1", "R2"))
            qbv = QBP[qb_].rearrange("p (j r) q -> p j r q", r=2)
            S.op('dve', lambda e: e.tensor_copy(out=qbv[0:64, :, 0, :], in_=QBT[0:64, :, tl]),
                 reads=[("QBT", g, m // 4) for g in range(4)], writes=[("qbp", qb_)], regions=("R1", "R2"))
            S.op('dve', lambda e: e.tensor_copy(out=qbv[64:128, :, 1, :], in_=QBT[64:128, :, tl]),
                 reads=[("QBT", g, m // 4) for g in range(4)], writes=[("qbp", qb_)], regions=("R1", "R2"))

        def att(m):
            M = hf * TPH + m
            qb_ = m % 2
            tl = slice(m * 128, (m + 1) * 128)
            if m == 0:
                pads(0)
            koffsA = [ko for ko in (-1, 0) if M + ko >= 0]
            nunits = 2 * len(koffsA) + 2 * len([ko for ko in (-4, -3, -2, -1, 0) if M + ko >= 0])
            ucnt = [0]

            def unit_done():
                ucnt[0] += 1
                if m >= 1 and ucnt[0] == max(1, nunits - 2):
                    tail_a(m - 1)
            for kv in range(2):
                for ko in koffsA:
                    ka, kb, va, vb, kk = kv_src(m, ko)
                    sb_ = SBANKS[pti[0] % 3]
                    pi = pti[0] % NPT
                    pti[0] += 1

                    def fnS(e, kv=kv, ka=ka, sb_=sb_, ko=ko):
                        e.matmul(PS[sb_][:], lhsT=IDENTB[:],
                                 rhs=EA[:, ko + 1, kv * 4:(kv + 1) * 4, :].rearrange("p h q -> p (h q)"),
                                 start=True, stop=False)
                        return e.matmul(PS[sb_][:], lhsT=ka, rhs=QAP[qb_][:, kv, :, :].rearrange("p g q -> p (g q)"),
                                        start=False, stop=True)
                    S.op('pe', fnS, reads=kk[0] + [("qap", qb_), ("ea",), ("identb",)], writes=[ps_key(sb_)],
                         regions=("R1", "R2"))
                    S.op('act', (lambda e, sb_=sb_, pi=pi: e.activation(out=PT[pi], in_=PS[sb_][:], func=AF.Exp, scale=0.125)),
                         reads=[ps_key(sb_)], writes=[("pt", pi)], regions=("R1",))
                    unit_done()
                    slots = [oslot(kv * 4 + g, m) for g in range(4)]

                    def fn(e, pi=pi, va=va, kv=kv, slots=slots):
                        ins = None
                        for g in range(4):
                            bk_, c_, st_ = slots[g]
                            ins = e.matmul(PS[bk_][:, c_:c_ + 65], lhsT=PT[pi][:, g * 128:(g + 1) * 128],
                                           rhs=va[:, kv * 65:(kv + 1) * 65], start=st_,
                                           stop=False, skip_group_check=True)
                        return ins
                    obs = sorted(set(s_[0] for s_ in slots))
                    defer(lambda fn=fn, pi=pi, kk=kk, obs=obs: S.op(
                        'pe', fn, reads=[("pt", pi)] + kk[2], writes=[ps_key(b_) for b_ in obs], regions=("R1", "R2")))
                    drain(DSKEW)
            if m + 1 < TPH:
                pads(m + 1)
            koffsB = [ko for ko in (-4, -3, -2, -1, 0) if M + ko >= 0]
            tyB = {0: 0, -1: 1, -2: 2, -3: 2, -4: 3}
            for ko in koffsB:
                ka, kb, va, vb, kk = kv_src(m, ko)
                for quad in range(2):
                    sb_ = SBANKS[pti[0] % 3]
                    pi = pti[0] % NPT
                    pti[0] += 1
                    ty = tyB[ko]

                    def fnS(e, kb=kb, quad=quad, sb_=sb_, ty=ty):
                        ins = e.matmul(PS[sb_][:], lhsT=IDENTB[:],
                                       rhs=EB[:, ty, quad * 4:(quad + 1) * 4, :].rearrange("p h q -> p (h q)"),
                                       start=True, stop=False)
                        for jj in range(2):
                            j = quad * 2 + jj
                            ins = e.matmul(PS[sb_][:, jj * 256:(jj + 1) * 256], lhsT=kb(j),
                                           rhs=QBP[qb_][:, 2 * j:2 * j + 2, :].rearrange("p h q -> p (h q)"),
                                           start=False, stop=(jj == 1))
                        return ins
                    S.op('pe', fnS, reads=kk[1] + [("qbp", qb_), ("eb",), ("identb",)], writes=[ps_key(sb_)],
                         regions=("R1", "R2"))
                    S.op('act', (lambda e, sb_=sb_, pi=pi: e.activation(out=PT[pi], in_=PS[sb_][:], func=AF.Exp, scale=0.125)),
                         reads=[ps_key(sb_)], writes=[("pt", pi)], regions=("R1",))
                    unit_done()
                    slots = [oslot(8 + 4 * quad + i, m) for i in range(4)]

                    def fn2(e, pi=pi, vb=vb, quad=quad, slots=slots):
                        ins = None
                        for i in range(4):
                            h = 4 * quad + i
                            bk_, c_, st_ = slots[i]
                            ins = e.matmul(PS[bk_][:, c_:c_ + 65], lhsT=PT[pi][:, i * 128:(i + 1) * 128],
                                           rhs=vb[:, h * 65:(h + 1) * 65], start=st_,
                                           stop=False, skip_group_check=True)
                        return ins
                    obs = sorted(set(s_[0] for s_ in slots))
                    defer(lambda fn2=fn2, pi=pi, kk=kk, obs=obs: S.op(
                        'pe', fn2, reads=[("pt", pi)] + kk[3], writes=[ps_key(b_) for b_ in obs], regions=("R1", "R2")))
                    drain(DSKEW)
            defer(lambda: post_tile(m))

        def finalize(m):
            yb = m % 2
            c = newst(16)
            stk = lambda c0, n: [("st", c0 + i) for i in range(n)]
            packs = ((2, 0, 7), (3, 7, 7), (4, 14, 2))
            for bk_, i0, nh in packs:
                ov = PS[bk_][:, 0:nh * 65].rearrange("p (h e) -> p h e", e=65)
                S.op('dve', (lambda e, ov=ov, i0=i0, nh=nh: e.tensor_copy(
                    out=ST[:, c + i0:c + i0 + nh], in_=ov[:, :, 64:65].rearrange("p h e -> p (h e)"))),
                    reads=[ps_key(bk_)], writes=stk(c + i0, nh))
            S.op('dve', lambda e: e.tensor_tensor(out=ST[:, c:c + 8], in0=ST[:, c:c + 8], in1=EXPS[:, 0:8], op=ALU.add),
                 reads=stk(c, 8) + [("exps",)], writes=stk(c, 8))
            c2 = newst(16)
            S.op('dve', lambda e: e.reciprocal(out=ST[:, c2:c2 + 16], in_=ST[:, c:c + 16]),
                 reads=stk(c, 16), writes=stk(c2, 16))
            for bk_, i0, nh in packs:
                ov = PS[bk_][:, 0:nh * 65].rearrange("p (h e) -> p h e", e=65)
                dst = XH[:, yb, i0 * 64:(i0 + nh) * 64].rearrange("p (h d) -> p h d", d=64)
                S.op('dve', (lambda e, ov=ov, dst=dst, i0=i0, nh=nh: e.tensor_tensor(
                    out=dst, in0=ov[:, :, 0:64],
                    in1=ST[:, c2 + i0:c2 + i0 + nh].unsqueeze(2).broadcast_to([128, nh, 64]), op=ALU.mult)),
                    reads=[ps_key(bk_)] + stk(c2 + i0, nh), writes=[("xh", yb)])
            c3 = newst(2)
            for grp in range(2):
                S.op('act', (lambda e, grp=grp: e.activation(out=JNK[:, 0:512], in_=XH[:, yb, grp * 512:(grp + 1) * 512],
                                                             func=AF.Square, accum_out=ST[:, c3 + grp:c3 + grp + 1])),
                     reads=[("xh", yb)], writes=[("sg", 0), ("st", c3 + grp)])
            c4 = newst(2)
            S.op('act', lambda e: e.activation(out=ST[:, c4:c4 + 2], in_=ST[:, c3:c3 + 2], func=AF.Ln,
                                               bias=EPSC[:, 0:1], scale=1.0 / 512),
                 reads=[("st", c3), ("st", c3 + 1), ("epsc",)], writes=[("st", c4), ("st", c4 + 1)])
            c5 = newst(2)
            S.op('act', lambda e: e.activation(out=ST[:, c5:c5 + 2], in_=ST[:, c4:c4 + 2], func=AF.Exp, scale=-0.5),
                 reads=[("st", c4), ("st", c4 + 1)], writes=[("st", c5), ("st", c5 + 1)])
            for grp in range(2):
                S.op('dve', (lambda e, grp=grp: e.tensor_scalar(out=YB[:, grp * 512:(grp + 1) * 512],
                                                                in0=XH[:, yb, grp * 512:(grp + 1) * 512],
                                                                scalar1=ST[:, c5 + grp:c5 + grp + 1], scalar2=None,
                                                                op0=ALU.mult)),
                     reads=[("xh", yb), ("st", c5 + grp)],
                     writes=[("yb", grp)])

        def tail(m):
            tail_a(m)
            tail_b(m)

        def tail_a(m):
            yb = m % 2

            def fn(e):
                ins = None
                for k in range(8):
                    ins = e.transpose(out=PSB[:, 6, k * 128:(k + 1) * 128],
                                      in_=YB[:, k * 128:(k + 1) * 128], identity=IDENTB[:])
                return ins
            S.op('pe', fn, reads=[("yb", 0), ("yb", 1), ("identb",)], writes=[ps_key(6)])
            S.op('dve', lambda e: e.tensor_tensor(
                out=HT[:, :, m * 128:(m + 1) * 128],
                in0=PSB[:, 6, :].rearrange("p (k t) -> p k t", t=128),
                in1=COLB[:, 61:69].unsqueeze(2).broadcast_to([128, 8, 128]), op=ALU.mult),
                reads=[ps_key(6), ("colb",)], writes=[("HT", m, kk) for kk in range(8)])

        def tail_b(m):
            for hh in range(2):
                idx = plan[("wout", hh)]
                slot = idx % NSLOT

                def fn2(e, hh=hh, slot=slot):
                    for k in range(8):
                        e.matmul(PS[6 + hh][:], lhsT=HT[:, k, m * 128:(m + 1) * 128], rhs=RING[:, slot, k, 0:512],
                                 start=(k == 0), stop=False)
                    return e.matmul(PS[6 + hh][:], lhsT=ONES[:, :], rhs=BHL[:, 640 + hh * 512:640 + (hh + 1) * 512],
                                    start=False, stop=True)
                S.op('pe', fn2, reads=ring_keys(idx) + [("HT", m, k) for k in range(8)] + [("ones",), ("bhi",), ("blo",)],
                     writes=[ps_key(6 + hh)])
            postnorm_residual(m, 6, 7, False, hf, lnexp=True)

        for m in range(TPH):
            if KMIX >= 2:
                att(m)
        drain(0)
        if KMIX >= 4:
            tail(TPH - 1)
        ring_release(plan[("wout", 0)])
        ring_release(plan[("wout", 1)])
        if hf == 0:
            S.op('dve', lambda e: e.tensor_copy(out=KAP[:], in_=KAT[:, 512:1024]),
                 reads=[("KAT", None, 1)], writes=[("kvp",)], regions=("R2",))
            S.op('dve', lambda e: e.tensor_copy(out=KBP[:], in_=KBT[:, :, 512:1024]),
                 reads=[("KBT", j, 1) for j in range(4)], writes=[("kvp",)], regions=("R2",))
            S.op('dve', lambda e: e.tensor_copy(out=VAP[:], in_=VA[:, 4:8, :]),
                 reads=[("VA", t) for t in range(4, 8)] + [("vaones",)], writes=[("kvp",)], regions=("R2",))
            S.op('dve', lambda e: e.tensor_copy(out=VBP[:], in_=VB[:, 4:8, :]),
                 reads=[("VB", t) for t in range(4, 8)] + [("vbones",)], writes=[("kvp",)], regions=("R2",))

    ring_issue_upto(NSLOT)
    setup()
    for hf in range(2):
        ada_mod(hf, 0)
        ada_mod(hf, 1)
        if hf == 0:
            make_gmod(0)
        if hf == 0 or stage < 3:
            prenorm(0)
        ffn(hf, 0, 0, final=(stage == 1))
        if stage >= 2:
            if hf == 0:
                ada_mod(hf, 3)
                ada_mod(hf, 4)
                make_gmod(1)
                prenorm(1)
            mixer(hf)
        if stage >= 3:
            ada_mod(hf, 6)
            ada_mod(hf, 7)
            if hf == 0:
                make_gmod(2)
            prenorm(2)
            ffn(hf, 1, 2, final=True)
        elif stage == 2:
            for t in range(TPH):
                r0 = (hf * TPH + t) * 128
                dma_sp(out_d[r0:r0 + 128, :], X[:, t, :], writes=[], reads=[("X", t)], ch=("o", t))
                if hf == 0:
                    r1 = (TPH + t) * 128
                    dma_sp(X[:, t, :], x_d[r1:r1 + 128, :], [("X", t)], ch=("x", t))

    sem_keys = sorted(S.cnt.keys(), key=str)
    sems = {}
    for i, k in enumerate(sem_keys):
        sems[k] = es.enter_context(nc.semaphore(f"s{i}"))

    def run_engine(e, name):
        for needs, fn, key, inc in S.ops[name]:
            for k, v in needs:
                e.wait_ge(sems[k], v)
            ins = fn(e)
            ins.then_inc(sems[key], inc)
        if name == 'sp':
            for k in sem_keys:
                e.wait_ge(sems[k], S.cnt[k])

    with nc.Block() as block:
        @block.tensor
        def _(e):
            run_engine(e, 'pe')

        @block.scalar
        def _(e):
            run_engine(e, 'act')

        @block.vector
        def _(e):
            run_engine(e, 'dve')

        @block.gpsimd
        def _(e):
            run_engine(e, 'pool')

        @block.sync
        def _(e):
            run_engine(e, 'sp')
    es.close()
    return nc


_NC_CACHE = {}


def _prep_inputs(inp):
    f = lambda a: np.ascontiguousarray(np.asarray(a, dtype=np.float32))
    perm = _win_perm()
    w_in = f(inp["w_in"])[0][:, perm]
    b_in = f(inp["b_in"])[0][perm]
    biasA, maskB, ident = _const_tables()
    rb = f(inp["rel_bias_b"])[0]
    s_ = np.arange(128)[:, None]
    q_ = np.arange(128)[None, :]
    rbT = np.zeros((128, 4, 8, 128), dtype=np.float32)
    for ty, koff in enumerate((0, -1, -2, -4)):
        idx = np.clip(q_ - s_ - koff * 128, -128, 128) + 128
        for slot, h in enumerate(HORDER):
            rbT[:, ty, slot, :] = rb[h][idx]
    g_pre = np.concatenate([f(inp["g_pre_ffn1"])[0], f(inp["g_pre_mix"])[0], f(inp["g_pre_ffn2"])[0]]).reshape(24, 128)
    g_post = np.concatenate([f(inp["g_post_ffn1"])[0], f(inp["g_post_mix"])[0], f(inp["g_post_ffn2"])[0]]).reshape(24, 128)
    b_qk = np.concatenate([b_in[0:1536], b_in[2048:2176]]).reshape(13, 128)
    b_row = np.concatenate([b_in[1536:2048], b_in[2176:2304], f(inp["b_out"])[0]]).reshape(1, 1664)
    g_grp = np.concatenate([f(inp["g_grp_a"])[0], f(inp["g_grp_b"])[0]]).reshape(8, 128)
    shared = {
        "w_ada": f(inp["w_ada"])[0], "b_ada": f(inp["b_ada"])[0].reshape(72, 128),
        "g_pre": np.ascontiguousarray(g_pre), "g_post": np.ascontiguousarray(g_post),
        "w_gate1": f(inp["w_gate1"])[0], "w_up1": f(inp["w_up1"])[0], "w_down1": f(inp["w_down1"])[0],
        "w_gate2": f(inp["w_gate2"])[0], "w_up2": f(inp["w_up2"])[0], "w_down2": f(inp["w_down2"])[0],
        "w_in": np.ascontiguousarray(w_in), "b_qk": np.ascontiguousarray(b_qk), "b_row": np.ascontiguousarray(b_row),
        "sinks": f(inp["sinks_a"]).reshape(1, 8), "rbT": np.ascontiguousarray(rbT.reshape(128, 4096)), "g_grp": np.ascontiguousarray(g_grp),
        "w_out": f(inp["w_out"])[0],
        "biasA": np.ascontiguousarray(biasA.reshape(128, 2048)),
        "maskB": np.ascontiguousarray(maskB.reshape(128, 512)), "ident": ident,
    }
    x = f(inp["x"])
    c = f(inp["c"])
    in_maps = []
    for b in range(8):
        m = dict(shared)
        m["x"] = np.ascontiguousarray(x[b])
        m["c"] = np.ascontiguousarray(c[b].reshape(8, 128))
        in_maps.append(m)
    return in_maps


def kernel(stage=3, **inputs):
    in_maps = _prep_inputs(inputs)
    if stage not in _NC_CACHE:
        _NC_CACHE[stage] = build_nc(stage)
    nc = _NC_CACHE[stage]
    res = run_bass_kernel_spmd(nc, in_maps, core_ids=list(range(8)))
    out = np.stack([np.asarray(r["out"], dtype=np.float32) for r in res.results], axis=0)
    return out
```

```python
import numpy as np
from contextlib import ExitStack
import concourse.bass as bass
import concourse.mybir as mybir
from concourse.bass_utils import run_bass_kernel_spmd

F32 = mybir.dt.float32
BF16 = mybir.dt.bfloat16
AF = mybir.ActivationFunctionType
ALU = mybir.AluOpType

D = 1024
SEQ = 2048
DFF = 2816
NJ = DFF // 128
TPH = 8
NSLOT = 3
EPS = 1e-6
NEGM = -30000.0
ENGS = ('pe', 'act', 'dve', 'pool', 'sp')


class Sched:
    def __init__(self):
        self.ops = {e: [] for e in ENGS}
        self.cnt = {}
        self.last_w = {}
        self.readers = {}
        self.known = {e: {} for e in ENGS}
        self.reg_prev = {}
        self.reg_cur = {}

    def phase(self, region):
        prev = self.reg_prev.setdefault(region, {})
        cur = self.reg_cur.setdefault(region, {})
        if cur:
            self.reg_prev[region] = dict(cur)
            self.reg_cur[region] = {}

    def _need(self, eng, ev, needs):
        key, val = ev
        if self.known[eng].get(key, 0) >= val:
            return
        if needs.get(key, 0) < val:
            needs[key] = val

    def op(self, eng, fn, reads=(), writes=(), dma=None, regions=()):
        needs = {}
        own = ('E', eng)
        for r in reads:
            ev = self.last_w.get(r)
            if ev is not None:
                if eng == 'pe' and ev[0] == own:
                    continue
                self._need(eng, ev, needs)
        strict = eng != 'pe'
        for w in writes:
            ev = self.last_w.get(w)
            if ev is not None and (strict or ev[0] != own):
                self._need(eng, ev, needs)
            for ev in self.readers.get(w, ()):
                if strict or ev[0] != own:
                    self._need(eng, ev, needs)
        for rg in regions:
            for k, v in self.reg_prev.get(rg, {}).items():
                if k != own:
                    self._need(eng, (k, v), needs)
        for k, v in needs.items():
            self.known[eng][k] = v
        if dma is None:
            key, inc = own, 1
        else:
            key, inc = ('D', dma), 16
        val = self.cnt.get(key, 0) + inc
        self.cnt[key] = val
        ev = (key, val)
        for w in writes:
            self.last_w[w] = ev
            self.readers[w] = []
        for r in reads:
            lst = self.readers.setdefault(r, [])
            lst[:] = [e for e in lst if e[0] != key]
            lst.append(ev)
        for rg in regions:
            cur = self.reg_cur.setdefault(rg, {})
            cur[key] = max(cur.get(key, 0), val)
        self.ops[eng].append((sorted(needs.items(), key=str), fn, key, inc))
        return ev


def _win_perm():
    QA0, KA0, VA0, QB0, KB0, VB0 = 0, 512, 640, 768, 1280, 1792
    idx = []
    for g in range(4):
        for kv in range(2):
            h = kv * 4 + g
            idx += list(range(QA0 + h * 64, QA0 + (h + 1) * 64))
    idx += list(range(QB0, QB0 + 512)) + list(range(KB0, KB0 + 512)) + list(range(VB0, VB0 + 512))
    idx += list(range(KA0, KA0 + 128)) + list(range(VA0, VA0 + 128))
    return np.array(idx, dtype=np.int64)


HORDER = [0, 1, 2, 3, 4, 5, 6, 7]


def _const_tables():
    s = np.arange(128)[:, None]
    q = np.arange(128)[None, :]
    qc = (q >= 64).astype(np.int64)
    sc = (s >= 64).astype(np.int64)
    slopes = np.array([2.0 ** (-8.0 * (i + 1) / 8) for i in range(8)], dtype=np.float64)
    biasA = np.zeros((128, 2, 8, 128), dtype=np.float32)
    for ko, koff in enumerate((-1, 0)):
        dist = np.abs(q - s - koff * 128).astype(np.float64)
        dchunk = qc - (2 * koff + sc)
        valid = (dchunk >= 0) & (dchunk <= 2)
        for h in range(8):
            b = -slopes[h] * dist
            biasA[:, ko, h, :] = np.where(valid, b, NEGM).astype(np.float32)
    maskB = np.zeros((128, 4, 128), dtype=np.float32)
    for ty, koff in enumerate((0, -1, -2, -4)):
        dchunk = qc - (2 * koff + sc)
        valid = (dchunk >= 0) & (dchunk <= 8)
        maskB[:, ty, :] = np.where(valid, 0.0, NEGM).astype(np.float32)
    ident = np.eye(128, dtype=np.float32)
    return biasA * 8.0, maskB * 8.0, ident


def build_nc(stage=3, dbg=False):
    nc = bass.Bass("TRN2", target_bir_lowering=False)

    def din(name, shape, dt=F32):
        return nc.dram_tensor(name, list(shape), dt, kind="ExternalInput")

    x_d = din("x", [SEQ, D]).ap()
    c_d = din("c", [8, 128]).ap()
    wada_d = din("w_ada", [D, 9 * D]).ap()
    bada_d = din("b_ada", [72, 128]).ap()
    gpre_d = din("g_pre", [24, 128]).ap()
    gpost_d = din("g_post", [24, 128]).ap()
    wg_d = [din("w_gate1", [D, DFF]).ap(), din("w_gate2", [D, DFF]).ap()]
    wu_d = [din("w_up1", [D, DFF]).ap(), din("w_up2", [D, DFF]).ap()]
    wd_d = [din("w_down1", [DFF, D]).ap(), din("w_down2", [DFF, D]).ap()]
    win_d = din("w_in", [D, 2304]).ap()
    bqk_d = din("b_qk", [13, 128]).ap()
    brow_d = din("b_row", [1, 1664]).ap()
    sinks_d = din("sinks", [1, 8]).ap()
    rbT_d = din("rbT", [128, 4096]).ap()
    ggrp_d = din("g_grp", [8, 128]).ap()
    wout_d = din("w_out", [D, D]).ap()
    biasA_d = din("biasA", [128, 2048]).ap()
    maskB_d = din("maskB", [128, 512]).ap()
    ident_d = din("ident", [128, 128]).ap()
    out_d = nc.dram_tensor("out", [SEQ, D], F32, kind="ExternalOutput").ap()

    es = ExitStack()

    def sb(name, shape, dt):
        return es.enter_context(nc.sbuf_tensor(name, list(shape), dt))

    X = sb("X", [128, TPH, D], F32)
    HT = sb("HT", [128, 8, 1024], BF16)
    R1 = sb("R1", [128, 11264], F32)
    R2 = sb("R2", [128, 11264], F32)
    RING = sb("RING", [128, NSLOT, 8, 512], BF16)
    KAP = sb("KAP", [128, 512], BF16)
    KBP = sb("KBP", [128, 4, 512], BF16)
    VAP = sb("VAP", [128, 4, 130], BF16)
    VBP = sb("VBP", [128, 4, 520], BF16)
    GG = sb("GG", [128, 1024], F32)
    EA = sb("EA", [128, 2, 8, 128], BF16)
    EB = sb("EB", [128, 4, 8, 128], BF16)
    XH = sb("XH", [128, 2, 1024], F32)
    SG = sb("SG", [128, 2, 512], F32)
    COLA = sb("COLA", [128, 80], F32)
    COLB = sb("COLB", [128, 69], F32)
    MODT = sb("MODT", [128, 72], F32)
    GMOD = sb("GMOD", [128, 3, 8], F32)
    IDENT = sb("IDENT", [128, 128], F32)
    IDENTB = sb("IDENTB", [128, 128], BF16)
    YB = sb("YB", [128, 1024], BF16)
    DG = sb("DG", [128, 2, 128], F32)
    SC = sb("SC", [128, 8], BF16)
    EXPS = sb("EXPS", [128, 8], F32)
    SINK = sb("SINK", [128, 8], F32)
    ST = sb("ST", [128, 128], F32)
    EPSC = sb("EPSC", [128, 1], F32)
    BHL = sb("BHL", [128, 1664], BF16)
    ONES = sb("ONES", [128, 128], BF16)

    PSALL = es.enter_context(nc.psum_tensor("psall", [128, 8, 512], F32))
    PS = [PSALL[:, i, :] for i in range(8)]
    PSB = PSALL[:].bitcast(BF16)

    JNK = SG[:].bitcast(BF16)[:, 0, :]
    VECA = R2[0:80, 2048:2176]
    VECB = R2[0:69, 2304:2432]
    BROW0 = R2[0:1, 0:1664]
    XHB = XH[:].bitcast(BF16)
    R1b = R1[:].bitcast(BF16)
    R2b = R2[:].bitcast(BF16)
    AT = R1b.rearrange("p (j t) -> p j t", t=1024)
    WD = R2b.rearrange("p (j t) -> p j t", t=1024)
    QAT = R2b[:, 0:4096].rearrange("p (g t) -> p g t", t=1024)
    QBT = R2b[:, 4096:8192].rearrange("p (g t) -> p g t", t=1024)
    KBT = R2b[:, 8192:12288].rearrange("p (g t) -> p g t", t=1024)
    KAT = R2b[:, 12288:13312]
    VA = R2b[:, 13312:14352].rearrange("p (t e) -> p t e", e=130)
    VB = R2b[:, 14352:18512].rearrange("p (t e) -> p t e", e=520)
    NPT = 6
    PT = [R1b[:, i * 512:(i + 1) * 512] for i in range(NPT)]
    QAP = [R1b[:, 4096 + i * 1024:4096 + (i + 1) * 1024].rearrange("p (a g q) -> p a g q", a=2, g=4) for i in range(2)]
    QBP = [R1b[:, 6144 + i * 1024:6144 + (i + 1) * 1024].rearrange("p (h q) -> p h q", q=128) for i in range(2)]
    SCRF = R1[:, 4096:8192]
    SCRM = R1[:, 8192:8704]
    SCRA = R1[:, 8704:10752]

    S = Sched()
    stcol = [0]

    def newst(n=1):
        c = stcol[0]
        if c + n > 128:
            c = 0
        stcol[0] = c + n
        return c

    ring_items = []
    ring_state = {'issued': 0}

    def ring_issue_upto(n):
        while ring_state['issued'] < min(n, len(ring_items)):
            i = ring_state['issued']
            slot = i % NSLOT
            for part, (src, c0, cw) in enumerate(ring_items[i]):
                S.op('pool',
                     (lambda e, slot=slot, src=src, c0=c0, cw=cw:
                      e.dma_start(out=RING[:, slot, :, c0:c0 + cw], in_=src)),
                     writes=[("ring", slot, part)], dma=("ring", slot, part))
            ring_state['issued'] = i + 1

    def ring_keys(i):
        assert ring_state['issued'] > i, (i, ring_state)
        return [("ring", i % NSLOT, p) for p in range(len(ring_items[i]))]

    def ring_release(i):
        ring_issue_upto(i + NSLOT + 1)

    def wsrc(w_ap, c0, cw):
        return w_ap.rearrange("(k p) f -> p k f", p=128)[:, :, c0:c0 + cw]

    def plan_half(hf):
        plan = {}

        def add(tag, parts):
            plan[tag] = len(ring_items)
            ring_items.append(parts)

        def ada(i):
            if hf == 0:
                for hh in range(2):
                    add(("ada", i, hh), [(wsrc(wada_d, i * 1024 + hh * 512, 512), 0, 512)])

        ada(0); ada(1)
        for f in range(2 if stage >= 3 else 1):
            if f == 1:
                ada(6); ada(7)
            for g in range(11):
                add(("gu", f, g), [(wsrc(wg_d[f], g * 256, 256), 0, 256),
                                   (wsrc(wu_d[f], g * 256, 256), 256, 256)])
            ada(2 if f == 0 else 8)
            if f == 0 and stage >= 2:
                ada(3); ada(4)
                for i in range(4):
                    add(("win", i), [(wsrc(win_d, i * 512, 512), 0, 512)])
                add(("win", 4), [(wsrc(win_d, 2048, 256), 0, 256)])
                ada(5)
                for i in range(2):
                    add(("wout", i), [(wsrc(wout_d, i * 512, 512), 0, 512)])
        return plan

    plans = [plan_half(0), plan_half(1)]

    def ps_key(i):
        return ("ps", i)

    def dma_sp(out, in_, writes, reads=(), ch=None, regions=()):
        S.op('sp', lambda e: e.dma_start(out=out, in_=in_), reads=reads, writes=writes,
             dma=ch, regions=regions)

    def setup():
        dma_sp(IDENT[:], ident_d, [("ident",)], ch=("c", 0))
        dma_sp(VECA[0:72, :], bada_d, [("veca", 0)], ch=("c", 1), regions=("R2",))
        dma_sp(VECA[72:80, :], c_d, [("veca", 1)], ch=("c", 2), regions=("R2",))
        dma_sp(VECB[0:24, :], gpre_d, [("vecb", 0)], ch=("c", 3), regions=("R2",))
        dma_sp(VECB[24:48, :], gpost_d, [("vecb", 1)], ch=("c", 4), regions=("R2",))
        dma_sp(VECB[48:61, :], bqk_d, [("vecb", 2)], ch=("c", 5), regions=("R2",))
        dma_sp(VECB[61:69, :], ggrp_d, [("vecb", 3)], ch=("c", 6), regions=("R2",))
        dma_sp(BROW0, brow_d, [("brow", 0)], ch=("c", 7), regions=("R2",))
        dma_sp(SINK[:], sinks_d.partition_broadcast(128), [("sink",)], ch=("c", 8))
        for t in range(TPH):
            dma_sp(X[:, t, :], x_d[t * 128:(t + 1) * 128, :], [("X", t)], ch=("x", t))
        S.op('pe', lambda e: e.transpose(out=PS[0][:, 0:80], in_=VECA, identity=IDENT[0:80, 0:80]),
             reads=[("veca", 0), ("veca", 1), ("ident",)], writes=[ps_key(0)], regions=("R2",))
        S.op('dve', lambda e: e.tensor_copy(out=COLA[:], in_=PS[0][:, 0:80]),
             reads=[ps_key(0)], writes=[("cola",)])
        S.op('pe', lambda e: e.transpose(out=PS[1][:, 0:69], in_=VECB, identity=IDENT[0:69, 0:69]),
             reads=[("vecb", i) for i in range(4)] + [("ident",)], writes=[ps_key(1)], regions=("R2",))
        S.op('dve', lambda e: e.tensor_copy(out=COLB[:], in_=PS[1][:, 0:69]),
             reads=[ps_key(1)], writes=[("colb",)])
        S.op('dve', lambda e: e.tensor_copy(out=IDENTB[:], in_=IDENT[:]), reads=[("ident",)], writes=[("identb",)])
        S.op('act', lambda e: e.activation(out=SC[:], in_=COLA[:, 72:80], func=AF.Silu),
             reads=[("cola",)], writes=[("sc",)])
        S.op('dve', lambda e: e.memset(ONES[:], 0.0), writes=[("ones",)])
        S.op('dve', lambda e: e.memset(ONES[0:2, :], 1.0), writes=[("ones",)])
        S.op('dve', lambda e: e.memset(BHL[:], 0.0), writes=[("bhi",), ("blo",)])
        S.op('dve', lambda e: e.memset(EPSC[:], EPS), writes=[("epsc",)])
        LO0 = R2[:].bitcast(BF16)[0:1, 8192:8192 + 1664]
        S.op('dve', lambda e: e.tensor_copy(out=BHL[0:1, :], in_=BROW0), reads=[("brow", 0)], writes=[("bhi",)],
             regions=("R2",))
        S.op('dve', lambda e: e.tensor_tensor(out=LO0, in0=BROW0, in1=BHL[0:1, :], op=ALU.subtract),
             reads=[("brow", 0), ("bhi",)], writes=[("lo0",)], regions=("R2",))
        dma_sp(BHL[1:2, :], LO0, [("blo",)], reads=[("lo0",)], ch=("c", 19), regions=("R2",))
        S.op('act', lambda e: e.activation(out=EXPS[:], in_=SINK[:], func=AF.Exp),
             reads=[("sink",)], writes=[("exps",)])

    def setup_tables():
        xk = []
        dma_sp(SCRA, biasA_d, [("scra",)], reads=xk, ch=("c", 9), regions=("R1",))
        dma_sp(SCRM, maskB_d, [("scrm",)], reads=xk, ch=("c", 10), regions=("R1",))
        dma_sp(SCRF, rbT_d, [("scrf",)], reads=xk, ch=("c", 11), regions=("R1",))

    def setup_tables_ops():
        S.op('dve', lambda e: e.tensor_copy(out=EA[:].rearrange("p a h q -> p (a h q)"),
                                            in_=SCRA),
             reads=[("scra",)], writes=[("ea",)], regions=("R1",))
        for ty in range(4):
            S.op('dve', (lambda e, ty=ty: e.scalar_tensor_tensor(
                out=EB[:, ty, :, :], in0=SCRF[:, ty * 1024:(ty + 1) * 1024].rearrange("p (h q) -> p h q", q=128), scalar=8.0,
                in1=SCRM[:, ty * 128:(ty + 1) * 128].unsqueeze(1).broadcast_to([128, 8, 128]),
                op0=ALU.mult, op1=ALU.add)),
                reads=[("scrm",), ("scrf",)], writes=[("eb",)], regions=("R1",))


    def ada_item(hf, i, hh):
        idx = plans[hf][("ada", i, hh)]
        slot = idx % NSLOT
        bank = 7

        def fn(e):
            ins = None
            for cc in range(4):
                for k in range(8):
                    ins = e.matmul(PS[bank][:, cc:cc + 1], lhsT=RING[:, slot, k, cc * 128:(cc + 1) * 128],
                                   rhs=SC[:, k:k + 1], start=(k == 0), stop=(k == 7))
            return ins
        S.op('pe', fn, reads=ring_keys(idx) + [("sc",)], writes=[ps_key(bank)])
        ring_release(idx)
        c0 = i * 8 + hh * 4
        S.op('dve', lambda e: e.tensor_tensor(out=MODT[:, c0:c0 + 4], in0=PS[bank][:, 0:4],
                                              in1=COLA[:, c0:c0 + 4], op=ALU.add),
             reads=[ps_key(bank), ("cola",)], writes=[("modt", i, hh)])

    def ada_mod(hf, i):
        if hf != 0:
            return
        for hh in range(2):
            ada_item(hf, i, hh)

    def make_gmod(s):
        S.op('dve', lambda e: e.scalar_tensor_tensor(out=GMOD[:, s, :], in0=MODT[:, (3 * s + 1) * 8:(3 * s + 1) * 8 + 8],
                                                     scalar=1.0, in1=COLB[:, 8 * s:8 * s + 8],
                                                     op0=ALU.add, op1=ALU.mult),
             reads=[("modt", 3 * s + 1, 0), ("modt", 3 * s + 1, 1), ("colb",)], writes=[("gmod", s)])

    def make_gg(s, wgt):
        gi = 3 * s + 2
        for kk in range(8):
            b = kk % 2
            S.op('dve', (lambda e, kk=kk, b=b: e.tensor_scalar(
                out=DG[:, b, :], in0=IDENT[:], scalar1=COLB[:, 24 + 8 * s + kk:24 + 8 * s + kk + 1],
                scalar2=None, op0=ALU.mult)),
                reads=[("ident",), ("colb",)], writes=[("dg", b)])
            bank = 6 + (kk // 4)
            S.op('pe', (lambda e, kk=kk, b=b, bank=bank: e.matmul(
                PS[bank][:, (kk % 4) * 128:(kk % 4 + 1) * 128],
                lhsT=MODT[:, gi * 8 + kk:gi * 8 + kk + 1].broadcast_to([128, 128]),
                rhs=DG[:, b, :], start=True, stop=True)),
                reads=[("dg", b), ("modt", gi, 0), ("modt", gi, 1)], writes=[ps_key(bank)])
        for hh in range(2):
            S.op('act', (lambda e, hh=hh: e.mul(out=GG[:, hh * 512:(hh + 1) * 512], in_=PS[6 + hh][:], mul=wgt)),
                 reads=[ps_key(6 + hh)], writes=[("gg", hh)])

    def prenorm(s):
        c = newst(8)
        for t in range(TPH):
            S.op('act', (lambda e, t=t: e.activation(out=JNK[:], in_=X[:, t, :], func=AF.Square,
                                                     accum_out=ST[:, c + t:c + t + 1])),
                 reads=[("X", t)], writes=[("sg", 0), ("st", c + t)])
        c2 = newst(8)
        S.op('act', lambda e: e.activation(out=ST[:, c2:c2 + 8], in_=ST[:, c:c + 8], func=AF.Sqrt,
                                           bias=EPSC[:, 0:1], scale=1.0 / D),
             reads=[("st", c + t) for t in range(8)] + [("epsc",)], writes=[("st", c2 + t) for t in range(8)])
        c3 = newst(8)
        S.op('dve', lambda e: e.reciprocal(out=ST[:, c3:c3 + 8], in_=ST[:, c2:c2 + 8]),
             reads=[("st", c2 + t) for t in range(8)], writes=[("st", c3 + t) for t in range(8)])
        for tb in range(2):
            for tt in range(4):
                t = tb * 4 + tt
                b = t % 2
                S.op('dve', (lambda e, t=t, b=b: e.tensor_scalar(
                    out=XHB[:, b, 0:1024], in0=X[:, t, :], scalar1=ST[:, c3 + t:c3 + t + 1], scalar2=None,
                    op0=ALU.mult)),
                    reads=[("X", t), ("st", c3 + t)], writes=[("xh", b)])

                def fn(e, tt=tt, b=b):
                    ins = None
                    for k in range(8):
                        ins = e.transpose(out=PSB[:, k, tt * 128:(tt + 1) * 128],
                                          in_=XHB[:, b, k * 128:(k + 1) * 128], identity=IDENTB[:])
                    return ins
                S.op('pe', fn, reads=[("xh", b), ("identb",)], writes=[ps_key(k) for k in range(8)])
            for k in range(8):
                wr = [("HT", tb * 4 + tt, k) for tt in range(4)]
                if k % 2 == 0:
                    S.op('act', (lambda e, k=k, tb=tb: e.activation(
                        out=HT[:, k, tb * 512:(tb + 1) * 512], in_=PSB[:, k, 0:512], func=AF.Identity,
                        bias=MODT[:, 3 * s * 8 + k:3 * s * 8 + k + 1], scale=GMOD[:, s, k:k + 1])),
                        reads=[ps_key(k), ("gmod", s), ("modt", 3 * s, 0), ("modt", 3 * s, 1)], writes=wr)
                else:
                    S.op('dve', (lambda e, k=k, tb=tb: e.tensor_scalar(
                        out=HT[:, k, tb * 512:(tb + 1) * 512], in0=PSB[:, k, 0:512], scalar1=GMOD[:, s, k:k + 1],
                        scalar2=MODT[:, 3 * s * 8 + k:3 * s * 8 + k + 1], op0=ALU.mult, op1=ALU.add)),
                        reads=[ps_key(k), ("gmod", s), ("modt", 3 * s, 0), ("modt", 3 * s, 1)], writes=wr)

    def prenorm_tile(s, t, bank, lnexp=False, dve_only=False):
        prenorm_front(t, lnexp)
        prenorm_back(s, t, bank, dve_only)

    def prenorm_seq(s, nbanks=4):
        for t in range(TPH):
            prenorm_front(t)
            if t >= 1:
                prenorm_back(s, t - 1, (t - 1) % nbanks)
        prenorm_back(s, TPH - 1, (TPH - 1) % nbanks)

    def prenorm_front(t, lnexp=False):
        c = newst(1)
        S.op('act', lambda e: e.activation(out=JNK[:], in_=X[:, t, :], func=AF.Square, accum_out=ST[:, c:c + 1]),
             reads=[("X", t)], writes=[("sg", 0), ("st", c)])
        c2 = newst(1)
        c3 = newst(1)
        if lnexp:
            S.op('act', lambda e: e.activation(out=ST[:, c2:c2 + 1], in_=ST[:, c:c + 1], func=AF.Ln,
                                               bias=EPSC[:, 0:1], scale=1.0 / D),
                 reads=[("st", c), ("epsc",)], writes=[("st", c2)])
            S.op('act', lambda e: e.activation(out=ST[:, c3:c3 + 1], in_=ST[:, c2:c2 + 1], func=AF.Exp, scale=-0.5),
                 reads=[("st", c2)], writes=[("st", c3)])
        else:
            S.op('act', lambda e: e.activation(out=ST[:, c2:c2 + 1], in_=ST[:, c:c + 1], func=AF.Sqrt,
                                               bias=EPSC[:, 0:1], scale=1.0 / D),
                 reads=[("st", c), ("epsc",)], writes=[("st", c2)])
            S.op('dve', lambda e: e.reciprocal(out=ST[:, c3:c3 + 1], in_=ST[:, c2:c2 + 1]),
                 reads=[("st", c2)], writes=[("st", c3)])
        b = t % 2
        S.op('dve', lambda e: e.tensor_scalar(out=XHB[:, b, 0:1024], in0=X[:, t, :], scalar1=ST[:, c3:c3 + 1],
                                              scalar2=None, op0=ALU.mult),
             reads=[("X", t), ("st", c3)], writes=[("xh", b)])

    def prenorm_back(s, t, bank, dve_only=False):
        b = t % 2

        def fn(e):
            ins = None
            for k in range(8):
                ins = e.transpose(out=PSB[:, bank, k * 128:(k + 1) * 128],
                                  in_=XHB[:, b, k * 128:(k + 1) * 128], identity=IDENTB[:])
            return ins
        S.op('pe', fn, reads=[("xh", b), ("identb",)], writes=[ps_key(bank)])
        for k in range(8):
            rd = [ps_key(bank), ("gmod", s), ("modt", 3 * s, 0), ("modt", 3 * s, 1)]
            if t % 2 == 0 and not dve_only:
                S.op('act', (lambda e, k=k: e.activation(
                    out=HT[:, k, t * 128:(t + 1) * 128], in_=PSB[:, bank, k * 128:(k + 1) * 128], func=AF.Identity,
                    bias=MODT[:, 3 * s * 8 + k:3 * s * 8 + k + 1], scale=GMOD[:, s, k:k + 1])),
                    reads=rd, writes=[("HT", t, k)])
            else:
                S.op('dve', (lambda e, k=k: e.tensor_scalar(
                    out=HT[:, k, t * 128:(t + 1) * 128], in0=PSB[:, bank, k * 128:(k + 1) * 128],
                    scalar1=GMOD[:, s, k:k + 1], scalar2=MODT[:, 3 * s * 8 + k:3 * s * 8 + k + 1],
                    op0=ALU.mult, op1=ALU.add)),
                    reads=rd, writes=[("HT", t, k)])

    def postnorm_residual(t, b0, b1, final, hf, lnexp=False, after=None):
        assert b1 == b0 + 1
        c = newst(1)
        S.op('act', lambda e: e.activation(out=JNK.rearrange("p (a b) -> p a b", a=2), in_=PSALL[:, b0:b0 + 2, :],
                                           func=AF.Square, accum_out=ST[:, c:c + 1]),
             reads=[ps_key(b0), ps_key(b1)], writes=[("sg", 0), ("st", c)])
        c2 = newst(1)
        c3 = newst(1)
        if lnexp:
            S.op('act', lambda e: e.activation(out=ST[:, c2:c2 + 1], in_=ST[:, c:c + 1], func=AF.Ln,
                                               bias=EPSC[:, 0:1], scale=1.0 / D),
                 reads=[("st", c), ("epsc",)], writes=[("st", c2)])
            S.op('act', lambda e: e.activation(out=ST[:, c3:c3 + 1], in_=ST[:, c2:c2 + 1], func=AF.Exp, scale=-0.5),
                 reads=[("st", c2)], writes=[("st", c3)])
        else:
            S.op('act', lambda e: e.activation(out=ST[:, c2:c2 + 1], in_=ST[:, c:c + 1], func=AF.Sqrt,
                                               bias=EPSC[:, 0:1], scale=1.0 / D),
                 reads=[("st", c), ("epsc",)], writes=[("st", c2)])
            S.op('dve', lambda e: e.reciprocal(out=ST[:, c3:c3 + 1], in_=ST[:, c2:c2 + 1]),
                 reads=[("st", c2)], writes=[("st", c3)])
        xb = t % 2
        S.op('dve', lambda e: e.scalar_tensor_tensor(
            out=XH[:, xb, :].rearrange("p (a b) -> p a b", a=2), in0=PSALL[:, b0:b0 + 2, :], scalar=ST[:, c3:c3 + 1],
            in1=GG[:].rearrange("p (a b) -> p a b", a=2), op0=ALU.mult, op1=ALU.mult),
            reads=[ps_key(b0), ps_key(b1), ("st", c3), ("gg", 0), ("gg", 1)], writes=[("xh", xb)])
        S.op('dve', lambda e: e.tensor_tensor(out=X[:, t, :], in0=X[:, t, :], in1=XH[:, xb, :], op=ALU.add),
             reads=[("X", t), ("xh", xb)], writes=[("X", t)])
        if final:
            r0 = (hf * TPH + t) * 128
            dma_sp(out_d[r0:r0 + 128, :], X[:, t, :], writes=[], reads=[("X", t)], ch=("o", t))
            if hf == 0:
                r1 = (TPH + t) * 128
                dma_sp(X[:, t, :], x_d[r1:r1 + 128, :], [("X", t)], ch=("x", t))
        if after is not None:
            after(t)

    PEND = {}

    def ffn(hf, f, s, final):
        plan = plans[hf]
        S.phase("R1")
        S.phase("R2")
        wdv = wd_d[f].rearrange("(j p) c -> p j c", p=128)
        for part in range(2):
            S.op('pool', (lambda e, part=part: e.dma_start(out=WD[:, part * 11:(part + 1) * 11, :],
                                                           in_=wdv[:, part * 11:(part + 1) * 11, :])),
                 writes=[("wd", part)], dma=("wd", part), regions=("R2",))
        if hf == 1:
            make_gg(s, 0.5)
        ui = 0
        for g in range(11):
            idx = plan[("gu", f, g)]
            slot = idx % NSLOT
            order = [(jj, tb) for jj in range(2) for tb in range(2)]
            hook = PEND.pop("back7", None) if (g == 0 and hf == 1 and f == 0) else None
            if hook is not None:
                order = [(0, 0), (1, 0), (0, 1), (1, 1)]
            for oi, (jj, tb) in enumerate(order):
                if hook is not None and oi == 2:
                    hook()
                j = g * 2 + jj
                if True:
                    bG = (ui % 4) * 2
                    bU = bG + 1
                    sgi = ui % 2
                    ui += 1

                    def fn(e, slot=slot, jj=jj, tb=tb, bG=bG, bU=bU):
                        ins = None
                        for k in range(8):
                            ins = e.matmul(PS[bG][:], lhsT=RING[:, slot, k, jj * 128:(jj + 1) * 128],
                                           rhs=HT[:, k, tb * 512:(tb + 1) * 512], start=(k == 0), stop=(k == 7))
                        for k in range(8):
                            ins = e.matmul(PS[bU][:], lhsT=RING[:, slot, k, 256 + jj * 128:256 + (jj + 1) * 128],
                                           rhs=HT[:, k, tb * 512:(tb + 1) * 512], start=(k == 0), stop=(k == 7))
                        return ins
                    S.op('pe', fn, reads=ring_keys(idx) + [("HT", tb * 4 + tt, k) for tt in range(4) for k in range(8)],
                         writes=[ps_key(bG), ps_key(bU)])
                    S.op('act', (lambda e, bG=bG, sgi=sgi: e.activation(out=SG[:, sgi, :], in_=PS[bG][:], func=AF.Silu)),
                         reads=[ps_key(bG)], writes=[("sg", sgi)])
                    S.op('dve', (lambda e, bU=bU, sgi=sgi, j=j, tb=tb: e.tensor_tensor(
                        out=AT[:, j, tb * 512:(tb + 1) * 512], in0=SG[:, sgi, :], in1=PS[bU][:], op=ALU.mult)),
                        reads=[("sg", sgi), ps_key(bU)], writes=[("AT", j, tb)], regions=("R1",))
            ring_release(idx)
        ada_mod(hf, 3 * s + 2)
        if hf == 0:
            make_gg(s, 0.5)
        after = None
        if f == 0 and stage >= 2 and hf == 1:
            nxt = 1
            after = lambda t: prenorm_front(t)
        elif final and hf == 0 and stage >= 3:
            nxt = 0
            after = "lag2"
        for t in range(TPH):
            banks = (0, 1) if t % 2 == 0 else (2, 3)
            for hh in range(2):
                bk = banks[hh]

                def fn(e, t=t, hh=hh, bk=bk):
                    ins = None
                    for j in range(NJ):
                        ins = e.matmul(PS[bk][:], lhsT=AT[:, j, t * 128:(t + 1) * 128],
                                       rhs=WD[:, j, hh * 512:(hh + 1) * 512], start=(j == 0), stop=(j == NJ - 1))
                    return ins
                S.op('pe', fn, reads=[("AT", j, t // 4) for j in range(NJ)] + [("wd", 0), ("wd", 1)],
                     writes=[ps_key(bk)], regions=("R1", "R2"))
            if after == "lag2":
                if t >= 2:
                    prenorm_back(nxt, t - 2, 4 + (t - 2) % 2)
                postnorm_residual(t, banks[0], banks[1], final, hf)
                if t >= 1:
                    prenorm_front(t - 1)
                continue
            postnorm_residual(t, banks[0], banks[1], final, hf, after=after)
            if after is not None and t >= 1:
                prenorm_back(nxt, t - 1, 4 + (t - 1) % 2)
        if after == "lag2":
            prenorm_back(nxt, TPH - 2, 4 + (TPH - 2) % 2)
            prenorm_front(TPH - 1)
            PEND["back7"] = lambda: prenorm_back(0, TPH - 1, 4 + (TPH - 1) % 2)
        elif after is not None:
            prenorm_back(nxt, TPH - 1, 4 + (TPH - 1) % 2)

    def mixer(hf):
        KMIX = 9
        plan = plans[hf]
        s = 1
        S.phase("R1")
        S.phase("R2")
        if hf == 0:
            setup_tables()
        HTk = lambda tb: [("HT", tb * 4 + tt, k) for tt in range(4) for k in range(8)]
        ev_i = [0]

        def evac_T(dst, bank, bcol, wkey):
            i = ev_i[0]
            ev_i[0] += 1
            if i % 2 == 0:
                S.op('act', lambda e: e.activation(out=dst, in_=PS[bank][:], func=AF.Identity,
                                                   bias=COLB[:, bcol:bcol + 1], scale=1.0),
                     reads=[ps_key(bank), ("colb",)], writes=[wkey], regions=("R2",))
            else:
                S.op('dve', lambda e: e.tensor_scalar(out=dst, in0=PS[bank][:], scalar1=COLB[:, bcol:bcol + 1],
                                                      scalar2=None, op0=ALU.add),
                     reads=[ps_key(bank), ("colb",)], writes=[wkey], regions=("R2",))

        bank_i = [0]

        def proj_T(idx, c0, dst3, gi, bcol, name):
            slot = idx % NSLOT
            for tb in range(2):
                bank = bank_i[0] % 4
                bank_i[0] += 1

                def fn(e, tb=tb, bank=bank):
                    ins = None
                    for k in range(8):
                        ins = e.matmul(PS[bank][:], lhsT=RING[:, slot, k, c0:c0 + 128],
                                       rhs=HT[:, k, tb * 512:(tb + 1) * 512], start=(k == 0), stop=(k == 7))
                    return ins
                S.op('pe', fn, reads=ring_keys(idx) + HTk(tb), writes=[ps_key(bank)])
                dst = dst3[:, gi, tb * 512:(tb + 1) * 512] if gi is not None else dst3[:, tb * 512:(tb + 1) * 512]
                evac_T(dst, bank, bcol, (name, gi, tb))

        S.op('dve', lambda e: e.memset(R1b[:, 4096:8192], 0.0), writes=[("qap", 0), ("qap", 1), ("qbp", 0), ("qbp", 1)],
             regions=("R1",))
        S.op('dve', lambda e: e.memset(VA.rearrange("p t (h e) -> p t h e", e=65)[:, :, :, 64:65], 1.0),
             writes=[("vaones",)], regions=("R2",))
        S.op('dve', lambda e: e.memset(VB.rearrange("p t (h e) -> p t h e", e=65)[:, :, :, 64:65], 1.0),
             writes=[("vbones",)], regions=("R2",))

        i0 = plan[("win", 0)]
        for g in range(4):
            proj_T(i0, g * 128, QAT, g, 48 + g, "QAT")
        ring_release(i0)
        i1 = plan[("win", 1)]
        for g in range(4):
            proj_T(i1, g * 128, QBT, g, 52 + g, "QBT")
        ring_release(i1)
        i2 = plan[("win", 2)]
        for g in range(4):
            proj_T(i2, g * 128, KBT, g, 56 + g, "KBT")
        ring_release(i2)
        i3 = plan[("win", 3)]
        slot3 = i3 % NSLOT
        for t in range(TPH):
            bank = 4 + t % 2

            def fn(e, t=t, bank=bank):
                for k in range(8):
                    e.matmul(PS[bank][:], lhsT=HT[:, k, t * 128:(t + 1) * 128], rhs=RING[:, slot3, k, 0:512],
                             start=(k == 0), stop=False)
                return e.matmul(PS[bank][:], lhsT=ONES[:, :], rhs=BHL[:, 0:512], start=False, stop=True)
            S.op('pe', fn, reads=ring_keys(i3) + [("HT", t, k) for k in range(8)] + [("ones",), ("bhi",), ("blo",)],
                 writes=[ps_key(bank)])
            S.op('act', (lambda e, t=t, bank=bank: e.copy(
                out=VB[:, t, :].rearrange("p (h e) -> p h e", e=65)[:, :, 0:64],
                in_=PS[bank][:].rearrange("p (h d) -> p h d", d=64))),
                reads=[ps_key(bank)], writes=[("VB", t)], regions=("R2",))
        ring_release(i3)
        i4 = plan[("win", 4)]
        slot4 = i4 % NSLOT
        proj_T(i4, 0, KAT, None, 60, "KAT")
        for t in range(TPH):
            bank = 4 + t % 2

            def fn(e, t=t, bank=bank):
                for k in range(8):
                    e.matmul(PS[bank][:, 0:128], lhsT=HT[:, k, t * 128:(t + 1) * 128],
                             rhs=RING[:, slot4, k, 128:256], start=(k == 0), stop=False)
                return e.matmul(PS[bank][:, 0:128], lhsT=ONES[:, :], rhs=BHL[:, 512:640], start=False, stop=True)
            S.op('pe', fn, reads=ring_keys(i4) + [("HT", t, k) for k in range(8)] + [("ones",), ("bhi",), ("blo",)],
                 writes=[ps_key(bank)])
            S.op('dve', (lambda e, t=t, bank=bank: e.tensor_copy(
                out=VA[:, t, :].rearrange("p (h e) -> p h e", e=65)[:, :, 0:64],
                in_=PS[bank][:, 0:128].rearrange("p (h d) -> p h d", d=64))),
                reads=[ps_key(bank)], writes=[("VA", t)], regions=("R2",))
        ring_release(i4)

        if hf == 0:
            setup_tables_ops()
        ada_mod(hf, 5)
        make_gg(1, 1.0)

        pti = [0]

        def kv_src(m, koff):
            mk = m + koff
            if mk >= 0:
                ka = KAT[:, mk * 128:(mk + 1) * 128]
                kb = lambda j: KBT[:, j, mk * 128:(mk + 1) * 128]
                va = VA[:, mk, :]
                vb = VB[:, mk, :]
                kk = [("KAT", None, mk // 4)], [("KBT", j, mk // 4) for j in range(4)], [("VA", mk), ("vaones",)], [("VB", mk), ("vbones",)]
            else:
                pk = 4 + mk
                ka = KAP[:, pk * 128:(pk + 1) * 128]
                kb = lambda j: KBP[:, j, pk * 128:(pk + 1) * 128]
                va = VAP[:, pk, :]
                vb = VBP[:, pk, :]
                kk = [("kvp",)], [("kvp",)], [("kvp",)], [("kvp",)]
            return ka, kb, va, vb, kk

        pending = []
        DSKEW = 3
        SBANKS = (0, 1, 5)
        started = set()

        def oslot(i, mtile):
            bank = 2 + i // 7
            st_ = (mtile, bank) not in started
            started.add((mtile, bank))
            return bank, (i % 7) * 65, st_

        def defer(th):
            pending.append(th)

        def drain(keep):
            while len(pending) > keep:
                pending.pop(0)()

        def post_tile(m):
            finalize(m)
            if m >= 1:
                tail_b(m - 1)

        def pads(m):
            qb_ = m % 2
            tl = slice(m * 128, (m + 1) * 128)
            S.op('dve', lambda e: e.tensor_copy(out=QAP[qb_][0:64, 0, :, :], in_=QAT[0:64, :, tl]),
                 reads=[("QAT", g, m // 4) for g in range(4)], writes=[("qap", qb_)], regions=("R1", "R2"))
            S.op('dve', lambda e: e.tensor_copy(out=QAP[qb_][64:128, 1, :, :], in_=QAT[64:128, :, tl]),
                 reads=[("QAT", g, m // 4) for g in range(4)], writes=[("qap", qb_)], regions=("R1", "R2"))
            qbv = QBP[qb_].rearrange("p (j r) q -> p j r q", r=2)
            S.op('dve', lambda e: e.tensor_copy(out=qbv[0:64, :, 0, :], in_=QBT[0:64, :, tl]),
                 reads=[("QBT", g, m // 4) for g in range(4)], writes=[("qbp", qb_)], regions=("R1", "R2"))
            S.op('dve', lambda e: e.tensor_copy(out=qbv[64:128, :, 1, :], in_=QBT[64:128, :, tl]),
                 reads=[("QBT", g, m // 4) for g in range(4)], writes=[("qbp", qb_)], regions=("R1", "R2"))

        def att(m):
            M = hf * TPH + m
            qb_ = m % 2
            tl = slice(m * 128, (m + 1) * 128)
            if m == 0:
                pads(0)
            koffsA = [ko for ko in (-1, 0) if M + ko >= 0]
            nunits = 2 * len(koffsA) + 2 * len([ko for ko in (-4, -3, -2, -1, 0) if M + ko >= 0])
            ucnt = [0]

            def unit_done():
                ucnt[0] += 1
                if m >= 1 and ucnt[0] == max(1, nunits - 3):
                    tail_a(m - 1)
            for kv in range(2):
                for ko in koffsA:
                    ka, kb, va, vb, kk = kv_src(m, ko)
                    sb_ = SBANKS[pti[0] % 3]
                    pi = pti[0] % NPT
                    pti[0] += 1

                    def fnS(e, kv=kv, ka=ka, sb_=sb_, ko=ko):
                        e.matmul(PS[sb_][:], lhsT=IDENTB[:],
                                 rhs=EA[:, ko + 1, kv * 4:(kv + 1) * 4, :].rearrange("p h q -> p (h q)"),
                                 start=True, stop=False)
                        return e.matmul(PS[sb_][:], lhsT=ka, rhs=QAP[qb_][:, kv, :, :].rearrange("p g q -> p (g q)"),
                                        start=False, stop=True)
                    S.op('pe', fnS, reads=kk[0] + [("qap", qb_), ("ea",), ("identb",)], writes=[ps_key(sb_)],
                         regions=("R1", "R2"))
                    S.op('act', (lambda e, sb_=sb_, pi=pi: e.activation(out=PT[pi], in_=PS[sb_][:], func=AF.Exp, scale=0.125)),
                         reads=[ps_key(sb_)], writes=[("pt", pi)], regions=("R1",))
                    unit_done()
                    slots = [oslot(kv * 4 + g, m) for g in range(4)]

                    def fn(e, pi=pi, va=va, kv=kv, slots=slots):
                        ins = None
                        for g in range(4):
                            bk_, c_, st_ = slots[g]
                            ins = e.matmul(PS[bk_][:, c_:c_ + 65], lhsT=PT[pi][:, g * 128:(g + 1) * 128],
                                           rhs=va[:, kv * 65:(kv + 1) * 65], start=st_,
                                           stop=False, skip_group_check=True)
                        return ins
                    obs = sorted(set(s_[0] for s_ in slots))
                    defer(lambda fn=fn, pi=pi, kk=kk, obs=obs: S.op(
                        'pe', fn, reads=[("pt", pi)] + kk[2], writes=[ps_key(b_) for b_ in obs], regions=("R1", "R2")))
                    drain(DSKEW)
            if m + 1 < TPH:
                pads(m + 1)
            koffsB = [ko for ko in (-4, -3, -2, -1, 0) if M + ko >= 0]
            tyB = {0: 0, -1: 1, -2: 2, -3: 2, -4: 3}
            for ko in koffsB:
                ka, kb, va, vb, kk = kv_src(m, ko)
                for quad in range(2):
                    sb_ = SBANKS[pti[0] % 3]
                    pi = pti[0] % NPT
                    pti[0] += 1
                    ty = tyB[ko]

                    def fnS(e, kb=kb, quad=quad, sb_=sb_, ty=ty):
                        ins = e.matmul(PS[sb_][:], lhsT=IDENTB[:],
                                       rhs=EB[:, ty, quad * 4:(quad + 1) * 4, :].rearrange("p h q -> p (h q)"),
                                       start=True, stop=False)
                        for jj in range(2):
                            j = quad * 2 + jj
                            ins = e.matmul(PS[sb_][:, jj * 256:(jj + 1) * 256], lhsT=kb(j),
                                           rhs=QBP[qb_][:, 2 * j:2 * j + 2, :].rearrange("p h q -> p (h q)"),
                                           start=False, stop=(jj == 1))
                        return ins
                    S.op('pe', fnS, reads=kk[1] + [("qbp", qb_), ("eb",), ("identb",)], writes=[ps_key(sb_)],
                         regions=("R1", "R2"))
                    S.op('act', (lambda e, sb_=sb_, pi=pi: e.activation(out=PT[pi], in_=PS[sb_][:], func=AF.Exp, scale=0.125)),
                         reads=[ps_key(sb_)], writes=[("pt", pi)], regions=("R1",))
                    unit_done()
                    slots = [oslot(8 + 4 * quad + i, m) for i in range(4)]

                    def fn2(e, pi=pi, vb=vb, quad=quad, slots=slots):
                        ins = None
                        for i in range(4):
                            h = 4 * quad + i
                            bk_, c_, st_ = slots[i]
                            ins = e.matmul(PS[bk_][:, c_:c_ + 65], lhsT=PT[pi][:, i * 128:(i + 1) * 128],
                                           rhs=vb[:, h * 65:(h + 1) * 65], start=st_,
                                           stop=False, skip_group_check=True)
                        return ins
                    obs = sorted(set(s_[0] for s_ in slots))
                    defer(lambda fn2=fn2, pi=pi, kk=kk, obs=obs: S.op(
                        'pe', fn2, reads=[("pt", pi)] + kk[3], writes=[ps_key(b_) for b_ in obs], regions=("R1", "R2")))
                    drain(DSKEW)
            defer(lambda: post_tile(m))

        def finalize(m):
            yb = m % 2
            c = newst(16)
            stk = lambda c0, n: [("st", c0 + i) for i in range(n)]
            packs = ((2, 0, 7), (3, 7, 7), (4, 14, 2))
            for bk_, i0, nh in packs:
                ov = PS[bk_][:, 0:nh * 65].rearrange("p (h e) -> p h e", e=65)
                S.op('dve', (lambda e, ov=ov, i0=i0, nh=nh: e.tensor_copy(
                    out=ST[:, c + i0:c + i0 + nh], in_=ov[:, :, 64:65].rearrange("p h e -> p (h e)"))),
                    reads=[ps_key(bk_)], writes=stk(c + i0, nh))
            S.op('dve', lambda e: e.tensor_tensor(out=ST[:, c:c + 8], in0=ST[:, c:c + 8], in1=EXPS[:, 0:8], op=ALU.add),
                 reads=stk(c, 8) + [("exps",)], writes=stk(c, 8))
            c2 = newst(16)
            S.op('dve', lambda e: e.reciprocal(out=ST[:, c2:c2 + 16], in_=ST[:, c:c + 16]),
                 reads=stk(c, 16), writes=stk(c2, 16))
            for bk_, i0, nh in packs:
                ov = PS[bk_][:, 0:nh * 65].rearrange("p (h e) -> p h e", e=65)
                dst = XH[:, yb, i0 * 64:(i0 + nh) * 64].rearrange("p (h d) -> p h d", d=64)
                S.op('dve', (lambda e, ov=ov, dst=dst, i0=i0, nh=nh: e.tensor_tensor(
                    out=dst, in0=ov[:, :, 0:64],
                    in1=ST[:, c2 + i0:c2 + i0 + nh].unsqueeze(2).broadcast_to([128, nh, 64]), op=ALU.mult)),
                    reads=[ps_key(bk_)] + stk(c2 + i0, nh), writes=[("xh", yb)])
            c3 = newst(2)
            for grp in range(2):
                S.op('act', (lambda e, grp=grp: e.activation(out=JNK[:, 0:512], in_=XH[:, yb, grp * 512:(grp + 1) * 512],
                                                             func=AF.Square, accum_out=ST[:, c3 + grp:c3 + grp + 1])),
                     reads=[("xh", yb)], writes=[("sg", 0), ("st", c3 + grp)])
            c4 = newst(2)
            S.op('act', lambda e: e.activation(out=ST[:, c4:c4 + 2], in_=ST[:, c3:c3 + 2], func=AF.Ln,
                                               bias=EPSC[:, 0:1], scale=1.0 / 512),
                 reads=[("st", c3), ("st", c3 + 1), ("epsc",)], writes=[("st", c4), ("st", c4 + 1)])
            c5 = newst(2)
            S.op('act', lambda e: e.activation(out=ST[:, c5:c5 + 2], in_=ST[:, c4:c4 + 2], func=AF.Exp, scale=-0.5),
                 reads=[("st", c4), ("st", c4 + 1)], writes=[("st", c5), ("st", c5 + 1)])
            for grp in range(2):
                S.op('dve', (lambda e, grp=grp: e.tensor_scalar(out=YB[:, grp * 512:(grp + 1) * 512],
                                                                in0=XH[:, yb, grp * 512:(grp + 1) * 512],
                                                                scalar1=ST[:, c5 + grp:c5 + grp + 1], scalar2=None,
                                                                op0=ALU.mult)),
                     reads=[("xh", yb), ("st", c5 + grp)],
                     writes=[("yb", grp)])

        def tail(m):
            tail_a(m)
            tail_b(m)

        def tail_a(m):
            yb = m % 2

            def fn(e):
                ins = None
                for k in range(8):
                    ins = e.transpose(out=PSB[:, 6, k * 128:(k + 1) * 128],
                                      in_=YB[:, k * 128:(k + 1) * 128], identity=IDENTB[:])
                return ins
            S.op('pe', fn, reads=[("yb", 0), ("yb", 1), ("identb",)], writes=[ps_key(6)])
            S.op('dve', lambda e: e.tensor_tensor(
                out=HT[:, :, m * 128:(m + 1) * 128],
                in0=PSB[:, 6, :].rearrange("p (k t) -> p k t", t=128),
                in1=COLB[:, 61:69].unsqueeze(2).broadcast_to([128, 8, 128]), op=ALU.mult),
                reads=[ps_key(6), ("colb",)], writes=[("HT", m, kk) for kk in range(8)])

        def tail_b(m):
            for hh in range(2):
                idx = plan[("wout", hh)]
                slot = idx % NSLOT

                def fn2(e, hh=hh, slot=slot):
                    for k in range(8):
                        e.matmul(PS[6 + hh][:], lhsT=HT[:, k, m * 128:(m + 1) * 128], rhs=RING[:, slot, k, 0:512],
                                 start=(k == 0), stop=False)
                    return e.matmul(PS[6 + hh][:], lhsT=ONES[:, :], rhs=BHL[:, 640 + hh * 512:640 + (hh + 1) * 512],
                                    start=False, stop=True)
                S.op('pe', fn2, reads=ring_keys(idx) + [("HT", m, k) for k in range(8)] + [("ones",), ("bhi",), ("blo",)],
                     writes=[ps_key(6 + hh)])
            postnorm_residual(m, 6, 7, False, hf, lnexp=True)

        for m in range(TPH):
            if KMIX >= 2:
                att(m)
        drain(0)
        if KMIX >= 4:
            tail(TPH - 1)
        ring_release(plan[("wout", 0)])
        ring_release(plan[("wout", 1)])
        if hf == 0:
            S.op('dve', lambda e: e.tensor_copy(out=KAP[:], in_=KAT[:, 512:1024]),
                 reads=[("KAT", None, 1)], writes=[("kvp",)], regions=("R2",))
            S.op('dve', lambda e: e.tensor_copy(out=KBP[:], in_=KBT[:, :, 512:1024]),
                 reads=[("KBT", j, 1) for j in range(4)], writes=[("kvp",)], regions=("R2",))
            S.op('dve', lambda e: e.tensor_copy(out=VAP[:], in_=VA[:, 4:8, :]),
                 reads=[("VA", t) for t in range(4, 8)] + [("vaones",)], writes=[("kvp",)], regions=("R2",))
            S.op('dve', lambda e: e.tensor_copy(out=VBP[:], in_=VB[:, 4:8, :]),
                 reads=[("VB", t) for t in range(4, 8)] + [("vbones",)], writes=[("kvp",)], regions=("R2",))

    ring_issue_upto(NSLOT)
    setup()
    for hf in range(2):
        ada_mod(hf, 0)
        ada_mod(hf, 1)
        if hf == 0:
            make_gmod(0)
        if hf == 0 or stage < 3:
            prenorm(0)
        ffn(hf, 0, 0, final=(stage == 1))
        if stage >= 2:
            if hf == 0:
                ada_mod(hf, 3)
                ada_mod(hf, 4)
                make_gmod(1)
                prenorm(1)
            mixer(hf)
        if stage >= 3:
            ada_mod(hf, 6)
            ada_mod(hf, 7)
            if hf == 0:
                make_gmod(2)
            prenorm(2)
            ffn(hf, 1, 2, final=True)
        elif stage == 2:
            for t in range(TPH):
                r0 = (hf * TPH + t) * 128
                dma_sp(out_d[r0:r0 + 128, :], X[:, t, :], writes=[], reads=[("X", t)], ch=("o", t))
                if hf == 0:
                    r1 = (TPH + t) * 128
                    dma_sp(X[:, t, :], x_d[r1:r1 + 128, :], [("X", t)], ch=("x", t))

    sem_keys = sorted(S.cnt.keys(), key=str)
    sems = {}
    for i, k in enumerate(sem_keys):
        sems[k] = es.enter_context(nc.semaphore(f"s{i}"))

    def run_engine(e, name):
        for needs, fn, key, inc in S.ops[name]:
            for k, v in needs:
                e.wait_ge(sems[k], v)
            ins = fn(e)
            ins.then_inc(sems[key], inc)
        if name == 'sp':
            for k in sem_keys:
                e.wait_ge(sems[k], S.cnt[k])

    with nc.Block() as block:
        @block.tensor
        def _(e):
            run_engine(e, 'pe')

        @block.scalar
        def _(e):
            run_engine(e, 'act')

        @block.vector
        def _(e):
            run_engine(e, 'dve')

        @block.gpsimd
        def _(e):
            run_engine(e, 'pool')

        @block.sync
        def _(e):
            run_engine(e, 'sp')
    es.close()
    return nc


_NC_CACHE = {}


def _prep_inputs(inp):
    f = lambda a: np.ascontiguousarray(np.asarray(a, dtype=np.float32))
    perm = _win_perm()
    w_in = f(inp["w_in"])[0][:, perm]
    b_in = f(inp["b_in"])[0][perm]
    biasA, maskB, ident = _const_tables()
    rb = f(inp["rel_bias_b"])[0]
    s_ = np.arange(128)[:, None]
    q_ = np.arange(128)[None, :]
    rbT = np.zeros((128, 4, 8, 128), dtype=np.float32)
    for ty, koff in enumerate((0, -1, -2, -4)):
        idx = np.clip(q_ - s_ - koff * 128, -128, 128) + 128
        for slot, h in enumerate(HORDER):
            rbT[:, ty, slot, :] = rb[h][idx]
    g_pre = np.concatenate([f(inp["g_pre_ffn1"])[0], f(inp["g_pre_mix"])[0], f(inp["g_pre_ffn2"])[0]]).reshape(24, 128)
    g_post = np.concatenate([f(inp["g_post_ffn1"])[0], f(inp["g_post_mix"])[0], f(inp["g_post_ffn2"])[0]]).reshape(24, 128)
    b_qk = np.concatenate([b_in[0:1536], b_in[2048:2176]]).reshape(13, 128)
    b_row = np.concatenate([b_in[1536:2048], b_in[2176:2304], f(inp["b_out"])[0]]).reshape(1, 1664)
    g_grp = np.concatenate([f(inp["g_grp_a"])[0], f(inp["g_grp_b"])[0]]).reshape(8, 128)
    shared = {
        "w_ada": f(inp["w_ada"])[0], "b_ada": f(inp["b_ada"])[0].reshape(72, 128),
        "g_pre": np.ascontiguousarray(g_pre), "g_post": np.ascontiguousarray(g_post),
        "w_gate1": f(inp["w_gate1"])[0], "w_up1": f(inp["w_up1"])[0], "w_down1": f(inp["w_down1"])[0],
        "w_gate2": f(inp["w_gate2"])[0], "w_up2": f(inp["w_up2"])[0], "w_down2": f(inp["w_down2"])[0],
        "w_in": np.ascontiguousarray(w_in), "b_qk": np.ascontiguousarray(b_qk), "b_row": np.ascontiguousarray(b_row),
        "sinks": f(inp["sinks_a"]).reshape(1, 8), "rbT": np.ascontiguousarray(rbT.reshape(128, 4096)), "g_grp": np.ascontiguousarray(g_grp),
        "w_out": f(inp["w_out"])[0],
        "biasA": np.ascontiguousarray(biasA.reshape(128, 2048)),
        "maskB": np.ascontiguousarray(maskB.reshape(128, 512)), "ident": ident,
    }
    x = f(inp["x"])
    c = f(inp["c"])
    in_maps = []
    for b in range(8):
        m = dict(shared)
        m["x"] = np.ascontiguousarray(x[b])
        m["c"] = np.ascontiguousarray(c[b].reshape(8, 128))
        in_maps.append(m)
    return in_maps


def kernel(stage=3, **inputs):
    in_maps = _prep_inputs(inputs)
    if stage not in _NC_CACHE:
        _NC_CACHE[stage] = build_nc(stage)
    nc = _NC_CACHE[stage]
    res = run_bass_kernel_spmd(nc, in_maps, core_ids=list(range(8)))
    out = np.stack([np.asarray(r["out"], dtype=np.float32) for r in res.results], axis=0)
    return out
```
